# Optimizing a Trainium2 kernel written in Bass

```python
import math
import jax, jax.numpy as jnp
from jax import lax
import numpy as np

D_MODEL = 1024
BATCH = 2
SEQ = 16384
DEPTH = 2
DEC_BATCH = 16
DEC_SEQ = 16
PAST_LEN = 2048

CHUNK = 64
H_A = 8
DA = 64
DVA = 2 * DA
H_B = 8
DB = 64
BAND_CHUNKS = 8
BAND_PAST = BAND_CHUNKS * CHUNK
BAND = BAND_PAST + CHUNK
REL_CLIP = 128
D_FF = 4 * D_MODEL
Q_BLOCK = 128
EPS = 1e-6
NEG_INF = -1e30
W_A_QK = H_A * 2 * DA
W_A_V = H_A * DVA
W_B = H_B * DB
IN_COLS = 2 * W_A_QK + W_A_V + 3 * W_B
SPLITS = (W_A_QK, 2 * W_A_QK, 2 * W_A_QK + W_A_V, 2 * W_A_QK + W_A_V + W_B, 2 * W_A_QK + W_A_V + 2 * W_B)

kernel_name = 'hybrid_diffattn_chunkband_stream_step'


def rms(x, g):
    xf = x.astype(jnp.float32)
    return xf * lax.rsqrt(jnp.mean(xf * xf, axis=-1, keepdims=True) + EPS) * g.astype(jnp.float32)


def alibi_slopes():
    return jnp.exp2(-8.0 * jnp.arange(1, H_A + 1, dtype=jnp.float32) / H_A)


def project(h, w_in, qn_a, kn_a, qn_b, kn_b):
    b, t = h.shape[:2]
    z = h @ w_in
    qa, ka, va, qb, kb, vb = jnp.split(z, SPLITS, axis=-1)
    qa = rms(qa.reshape(b, t, H_A, 2, DA), qn_a)
    ka = rms(ka.reshape(b, t, H_A, 2, DA), kn_a).reshape(b, t, H_A, 2 * DA)
    va = va.reshape(b, t, H_A, DVA)
    qb = rms(qb.reshape(b, t, H_B, DB), qn_b)
    kb = rms(kb.reshape(b, t, H_B, DB), kn_b)
    vb = vb.reshape(b, t, H_B, DB)
    return qa, ka, va, qb, kb, vb


def diff_attention(qa, k, v, q_pos, k_pos, lam):
    q1 = qa[..., 0, :]
    q2 = qa[..., 1, :]
    k1 = k[..., :DA]
    k2 = k[..., DA:]
    visible = (k_pos[None, :] // CHUNK) <= (q_pos[:, None] // CHUNK)
    dist = jnp.abs(q_pos[:, None] - k_pos[None, :]).astype(jnp.float32)
    bias = jnp.where(visible[None], -alibi_slopes()[:, None, None] * dist[None], NEG_INF)
    scale = DA ** -0.5
    p1 = jax.nn.softmax(jnp.einsum('bqhd,bkhd->bhqk', q1, k1) * scale + bias, axis=-1)
    p2 = jax.nn.softmax(jnp.einsum('bqhd,bkhd->bhqk', q2, k2) * scale + bias, axis=-1)
    return jnp.einsum('bhqk,bkhe->bqhe', p1 - lam * p2, v)


def band_attention(q, k, v, q_pos, k_pos, rel_bias):
    dchunk = q_pos[:, :, None] // CHUNK - k_pos[:, None, :] // CHUNK
    visible = (k_pos[:, None, :] >= 0) & (dchunk >= 0) & (dchunk <= BAND_CHUNKS)
    rel = jnp.clip(q_pos[:, :, None] - k_pos[:, None, :], -REL_CLIP, REL_CLIP) + REL_CLIP
    bias = jnp.where(visible[None], rel_bias.astype(jnp.float32)[:, rel], NEG_INF)
    s = jnp.einsum('bnqhd,bnkhd->bhnqk', q, k) * (DB ** -0.5) + bias
    p = jax.nn.softmax(s, axis=-1)
    return jnp.einsum('bhnqk,bnkhd->bnqhd', p, v)


def merge_and_ffn(x, h, oa, ob, lam_init, subln_g, w_br_a, w_br_b, w_gate, w_out, norm2_g, w_ff1, w_ff2):
    b, t = x.shape[:2]
    ya = (rms(oa, subln_g) * (1.0 - lam_init)).reshape(b, t, W_A_V) @ w_br_a
    yb = ob.reshape(b, t, W_B) @ w_br_b
    ga, gb = jnp.split(jax.nn.sigmoid(h @ w_gate), 2, axis=-1)
    x = x + (ga * ya + gb * yb) @ w_out
    u = jnp.maximum(rms(x, norm2_g) @ w_ff1, 0.0)
    return x + (u * u) @ w_ff2


def setup_inputs(seed: int = 0) -> dict:
    key = jax.random.key(seed)
    ks = jax.random.split(key, 25)
    f32 = jnp.float32
    nrm = lambda k, shape, scale: jax.random.normal(k, shape, f32) * scale
    gain = lambda k, shape: 1.0 + 0.02 * jax.random.normal(k, shape, f32)
    b_keep = min(BAND_PAST, PAST_LEN)
    return {
        'x_prompt': nrm(ks[0], (BATCH, SEQ, D_MODEL), 1.0),
        'x_sample': nrm(ks[1], (DEC_BATCH, DEC_SEQ, D_MODEL), 1.0),
        'cache_a_k': nrm(ks[2], (DEPTH, DEC_BATCH, PAST_LEN, H_A, 2 * DA), 1.0),
        'cache_a_v': nrm(ks[3], (DEPTH, DEC_BATCH, PAST_LEN, H_A, DVA), 1.0),
        'cache_b_k': nrm(ks[4], (DEPTH, DEC_BATCH, b_keep, H_B, DB), 1.0),
        'cache_b_v': nrm(ks[5], (DEPTH, DEC_BATCH, b_keep, H_B, DB), 1.0),
        'norm1_g': gain(ks[6], (DEPTH, D_MODEL)),
        'w_in': nrm(ks[7], (DEPTH, D_MODEL, IN_COLS), D_MODEL ** -0.5),
        'qn_a_g': gain(ks[8], (DEPTH, DA)),
        'kn_a_g': gain(ks[9], (DEPTH, DA)),
        'qn_b_g': gain(ks[10], (DEPTH, DB)),
        'kn_b_g': gain(ks[11], (DEPTH, DB)),
        'lam_q1': nrm(ks[12], (DEPTH, DA), 0.1),
        'lam_k1': nrm(ks[13], (DEPTH, DA), 0.1),
        'lam_q2': nrm(ks[14], (DEPTH, DA), 0.1),
        'lam_k2': nrm(ks[15], (DEPTH, DA), 0.1),
        'subln_a_g': gain(ks[16], (DEPTH, DVA)),
        'rel_bias_b': nrm(ks[17], (DEPTH, H_B, 2 * REL_CLIP + 1), 0.5),
        'w_br_a': nrm(ks[18], (DEPTH, W_A_V, D_MODEL), W_A_V ** -0.5),
        'w_br_b': nrm(ks[19], (DEPTH, W_B, D_MODEL), W_B ** -0.5),
        'w_gate': nrm(ks[20], (DEPTH, D_MODEL, 2 * D_MODEL), D_MODEL ** -0.5),
        'w_out': nrm(ks[21], (DEPTH, D_MODEL, D_MODEL), D_MODEL ** -0.5),
        'norm2_g': gain(ks[22], (DEPTH, D_MODEL)),
        'w_ff1': nrm(ks[23], (DEPTH, D_MODEL, D_FF), D_MODEL ** -0.5),
        'w_ff2': nrm(ks[24], (DEPTH, D_FF, D_MODEL), D_FF ** -0.5),
    }


def reference(x_prompt, x_sample, cache_a_k, cache_a_v, cache_b_k, cache_b_v,
              norm1_g, w_in, qn_a_g, kn_a_g, qn_b_g, kn_b_g,
              lam_q1, lam_k1, lam_q2, lam_k2, subln_a_g, rel_bias_b,
              w_br_a, w_br_b, w_gate, w_out, norm2_g, w_ff1, w_ff2):
    f32 = jnp.float32
    bp, s = x_prompt.shape[:2]
    t = x_sample.shape[1]
    p_len = cache_a_k.shape[2]
    b_keep = cache_b_k.shape[2]
    nb = s // Q_BLOCK
    nc = s // CHUNK
    keep_p = min(BAND_PAST, s)

    pos_p = jnp.arange(s)
    pos_blocks = pos_p.reshape(nb, Q_BLOCK)
    band_idx = jnp.arange(nc)[:, None] * CHUNK + jnp.arange(BAND)[None, :]
    band_kpos = band_idx - BAND_PAST
    pos_chunks = pos_p.reshape(nc, CHUNK)
    q_pos_s = p_len + jnp.arange(t)
    k_pos_a_s = jnp.arange(p_len + t)
    k_pos_b_s = jnp.arange(p_len - b_keep, p_len + t)
    pad_rows = lambda r: jnp.pad(r, ((0, 0), (BAND_PAST, 0), (0, 0), (0, 0)))

    xp = x_prompt.astype(f32)
    xs = x_sample.astype(f32)
    ak_p, av_p, bk_p, bv_p = [], [], [], []
    ak_s, av_s, bk_s, bv_s = [], [], [], []
    for l in range(DEPTH):
        lam_init = 0.8 - 0.6 * math.exp(-0.3 * l)
        lam = (jnp.exp(jnp.sum(lam_q1[l].astype(f32) * lam_k1[l].astype(f32)))
               - jnp.exp(jnp.sum(lam_q2[l].astype(f32) * lam_k2[l].astype(f32))) + lam_init)
        w_in_l = w_in[l].astype(f32)
        tail = functools_partial_args = (lam_init, subln_a_g[l], w_br_a[l].astype(f32), w_br_b[l].astype(f32),
                                         w_gate[l].astype(f32), w_out[l].astype(f32), norm2_g[l],
                                         w_ff1[l].astype(f32), w_ff2[l].astype(f32))

        hp = rms(xp, norm1_g[l])
        qa, ka, va, qb, kb, vb = project(hp, w_in_l, qn_a_g[l], kn_a_g[l], qn_b_g[l], kn_b_g[l])
        qa_blocks = qa.reshape(bp, nb, Q_BLOCK, H_A, 2, DA).swapaxes(0, 1)
        oa = lax.map(lambda a: diff_attention(a[0], ka, va, a[1], pos_p, lam), (qa_blocks, pos_blocks))
        oa = oa.swapaxes(0, 1).reshape(bp, s, H_A, DVA)
        ob = band_attention(qb.reshape(bp, nc, CHUNK, H_B, DB), pad_rows(kb)[:, band_idx],
                            pad_rows(vb)[:, band_idx], pos_chunks, band_kpos, rel_bias_b[l])
        ob = ob.reshape(bp, s, H_B, DB)
        xp = merge_and_ffn(xp, hp, oa, ob, *tail)
        ak_p.append(ka)
        av_p.append(va)
        bk_p.append(kb[:, s - keep_p:])
        bv_p.append(vb[:, s - keep_p:])

        hs = rms(xs, norm1_g[l])
        qa2, ka2, va2, qb2, kb2, vb2 = project(hs, w_in_l, qn_a_g[l], kn_a_g[l], qn_b_g[l], kn_b_g[l])
        ka_all = jnp.concatenate([cache_a_k[l].astype(f32), ka2], axis=1)
        va_all = jnp.concatenate([cache_a_v[l].astype(f32), va2], axis=1)
        oa2 = diff_attention(qa2, ka_all, va_all, q_pos_s, k_pos_a_s, lam)
        kb_all = jnp.concatenate([cache_b_k[l].astype(f32), kb2], axis=1)
        vb_all = jnp.concatenate([cache_b_v[l].astype(f32), vb2], axis=1)
        ob2 = band_attention(qb2[:, None], kb_all[:, None], vb_all[:, None],
                             q_pos_s[None], k_pos_b_s[None], rel_bias_b[l])[:, 0]
        xs = merge_and_ffn(xs, hs, oa2, ob2, *tail)
        ak_s.append(ka2)
        av_s.append(va2)
        bk_s.append(kb2)
        bv_s.append(vb2)

    cdt = cache_a_k.dtype
    y_prompt = xp.astype(x_prompt.dtype)
    y_sample = xs.astype(x_sample.dtype)
    return (y_prompt, y_sample,
            jnp.stack(ak_p).astype(cdt), jnp.stack(av_p).astype(cdt),
            jnp.stack(bk_p).astype(cdt), jnp.stack(bv_p).astype(cdt),
            jnp.stack(ak_s).astype(cdt), jnp.stack(av_s).astype(cdt),
            jnp.stack(bk_s).astype(cdt), jnp.stack(bv_s).astype(cdt))
```

```python
import contextlib
import math
import os

import numpy as np
import ml_dtypes

import concourse.bass as bass
import concourse.mybir as mybir
from concourse.bass_utils import run_bass_kernel_spmd

F32 = mybir.dt.float32
BF16 = mybir.dt.bfloat16
AF = mybir.ActivationFunctionType
ALU = mybir.AluOpType
AX = mybir.AxisListType
EPS = 1e-6
NC = 8
D = 1024
DFF = 4096


class _Op:
    __slots__ = ("eng", "fn", "deps", "kind", "sig", "signal", "slot", "slotcnt", "idx")


class _Rec:
    def __init__(self):
        self.calls = []

    def __getattr__(self, name):
        def f(*a, **k):
            self.calls.append((name, a, k))
            return self
        return f


class Prog:
    CAP = 1500
    RING = {"sp": 16, "pool": 16, "act": 4}
    COMPUTE = ("pe", "act", "dve", "pool")

    def __init__(self, nc):
        self.nc = nc
        self.ops = []
        self.last_w = {}
        self.readers = {}
        self.ndma = {"sp": 0, "pool": 0, "act": 0}

    def add(self, eng, fn, r=(), w=(), kind="c", late=False):
        if not late:
            rec = _Rec()
            fn(rec)
            assert len(rec.calls) == 1
            fn = (lambda nm, a, k: (lambda E: getattr(E, nm)(*a, **k)))(*rec.calls[0])
        op = _Op()
        op.idx = len(self.ops)
        op.eng, op.fn, op.kind = eng, fn, kind
        op.signal = kind != "c"
        op.sig = None
        r = tuple(r) + ("ARENA",)
        deps = set()
        for k in r + tuple(w):
            lw = self.last_w.get(k)
            if lw is not None:
                deps.add(lw)
        for k in w:
            rd = self.readers.get(k)
            if rd:
                deps.update(rd.values())
        for k in w:
            self.last_w[k] = op.idx
            self.readers[k] = {}
        for k in r:
            d = self.readers.setdefault(k, {})
            if kind == "c":
                d[("c", eng)] = op.idx
            else:
                d[("d", op.idx)] = op.idx
        deps.discard(op.idx)
        best = {}
        out = []
        for i in deps:
            o = self.ops[i]
            if o.kind == "c":
                if o.eng == "pe" and eng == "pe" and kind == "c":
                    continue
                if best.get(o.eng, -1) < i:
                    best[o.eng] = i
            else:
                out.append(i)
        out.extend(best.values())
        out.sort(reverse=True)
        op.deps = out
        for i in out:
            self.ops[i].signal = True
        if kind == "d":
            n = self.ndma[eng]
            self.ndma[eng] = n + 1
            R = self.RING[eng]
            op.slot = n % R
            op.slotcnt = n // R + 1
        self.ops.append(op)
        return op.idx

    def pe(self, fn, r=(), w=()):
        return self.add("pe", fn, r, w)

    def act(self, fn, r=(), w=()):
        return self.add("act", fn, r, w)

    def dve(self, fn, r=(), w=()):
        return self.add("dve", fn, r, w)

    def pool(self, fn, r=(), w=()):
        return self.add("pool", fn, r, w)

    def dma(self, q, out, in_, r=(), w=()):
        return self.add(q, lambda e: e.dma_start(out=out, in_=in_), r, w, kind="d")

    def barrier(self, scratch):
        self.add("pool", lambda e: e.memset(scratch, 0.0), r=(), w=("ARENA", "barrier_scratch"))

    def emit(self):
        nc = self.nc
        ops = self.ops
        cnt = {e: 0 for e in self.COMPUTE}
        for op in ops:
            if op.kind == "c" and op.signal:
                c = cnt[op.eng]
                cnt[op.eng] = c + 1
                op.sig = (op.eng, c // self.CAP, c % self.CAP + 1)
        sems = {}
        with contextlib.ExitStack() as st:
            for e in self.COMPUTE:
                for ep in range(cnt[e] // self.CAP + 1):
                    sems[(e, ep)] = st.enter_context(nc.semaphore(f"s_{e}_{ep}"))
            for q, R in self.RING.items():
                for s in range(min(R, self.ndma[q])):
                    sems[("dma", q, s)] = st.enter_context(nc.semaphore(f"d_{q}_{s}"))
            k = 0
            for op in ops:
                if op.kind == "cc":
                    sems[("cc", op.idx)] = st.enter_context(nc.semaphore(f"cc_{k}"))
                    k += 1
            block = st.enter_context(nc.Block())
            per_eng = {e: [] for e in ("pe", "act", "dve", "pool", "sp")}
            for op in ops:
                per_eng[op.eng].append(op)

            def target(op):
                if op.kind == "c":
                    e, ep, v = op.sig
                    return ("c", e), sems[(e, ep)], (ep, v)
                if op.kind == "d":
                    return ("d", op.eng, op.slot), sems[("dma", op.eng, op.slot)], (0, 16 * op.slotcnt)
                return ("cc", op.idx), sems[("cc", op.idx)], (0, 1)

            def run(engname, E):
                seen = {}
                if engname == "sp":
                    self.pid = E.partition_id()
                for op in per_eng[engname]:
                    for i in op.deps:
                        key, sem, val = target(ops[i])
                        if seen.get(key, (-1, -1)) >= val:
                            continue
                        E.wait_ge(sem, val[1])
                        seen[key] = val
                    if op.kind == "d" and op.slotcnt > 1:
                        key = ("d", op.eng, op.slot)
                        val = (0, 16 * (op.slotcnt - 1))
                        if seen.get(key, (-1, -1)) < val:
                            E.wait_ge(sems[("dma", op.eng, op.slot)], val[1])
                            seen[key] = val
                    ins = op.fn(E)
                    if op.kind == "c":
                        if op.signal:
                            ins.then_inc(sems[(op.sig[0], op.sig[1])], 1)
                    elif op.kind == "d":
                        ins.then_inc(sems[("dma", op.eng, op.slot)], 16)
                    else:
                        ins.then_inc(sems[("cc", op.idx)], 1)
                if engname == "sp":
                    for q, R in self.RING.items():
                        n = self.ndma[q]
                        for s in range(min(R, n)):
                            c = (n - 1 - s) // R + 1
                            E.wait_ge(sems[("dma", q, s)], 16 * c)
                    for op in ops:
                        if op.kind == "cc":
                            E.wait_ge(sems[("cc", op.idx)], 1)

            @block.tensor
            def _(E):
                run("pe", E)

            @block.scalar
            def _(E):
                run("act", E)

            @block.vector
            def _(E):
                run("dve", E)

            @block.gpsimd
            def _(E):
                run("pool", E)

            @block.sync
            def _(E):
                run("sp", E)


class SBAlloc:
    def __init__(self, nc, base=16512, limit=229376):
        self.nc, self.off, self.limit = nc, base, limit

    def __call__(self, name, shape, dt):
        n = 1
        for s in shape[1:]:
            n *= s
        size = n * (4 if dt == F32 else 2)
        size = (size + 63) // 64 * 64
        t = self.nc.alloc_sbuf_tensor_at(name, list(shape), dt, offset=self.off)
        self.off += size
        assert self.off <= self.limit, (name, self.off)
        return t


def build(cfg):
    S, PAST, L = cfg["SEQ"], cfg["PAST"], cfg["L"]
    NTOK = 2 * S
    TPC = NTOK // NC
    NST = NTOK // 512
    STB = S // 512
    NKT = S // 128
    MYT = TPC + 32
    NALL = NTOK + 256
    NCT = PAST // 128
    TPR = TPC // 512
    assert TPC % 512 == 0 and PAST % 1024 == 0 or PAST == 512

    nc = bass.Bass("TRN2", target_bir_lowering=False)

    def din(name, shape, dt=F32):
        return nc.dram_tensor(name, list(shape), dt, kind="ExternalInput").ap()

    def dout(name, shape, dt=F32):
        return nc.dram_tensor(name, list(shape), dt, kind="ExternalOutput").ap()

    def dint(name, shape, dt):
        return nc.dram_tensor(name, list(shape), dt)

    xp = din("xp", [TPC, D]); xs = din("xs", [32, D])
    win = din("win", [L, D, 576])
    g1b = din("g1b", [L, 128, D]); g2b = din("g2b", [L, 128, D])
    qkg = din("qkg", [L, 128, 512]); lamv = din("lamv", [L, 128, 256]); sublng = din("sublng", [L, 128, 1])
    bblk = din("bblk", [L, 128, 11 * 128]); bblks = din("bblks", [L, 128, 80])
    cak = din("cak", [L, 16, PAST, 128]); cav = din("cav", [L, 16, PAST, 128])
    cbk = din("cbk", [L, 16, 512, 64]); cbv = din("cbv", [L, 16, 512, 64])
    wbra = din("wbra", [L, D, D]); wbrb = din("wbrb", [L, 512, D]); wgate = din("wgate", [L, D, 2 * D])
    wout = din("wout", [L, D, D]); wff1 = din("wff1", [L, D, DFF]); wff2 = din("wff2", [L, DFF, D])
    kaugS = din("kaugS", [3, S], BF16); qaug = din("qaug", [3, 512], BF16); qaugs = din("qaugs", [3, 256], BF16)
    biasA_d = din("biasA", [128, NKT + 3]); cblk_d = din("cblk", [128, 7 * 128], BF16)
    identb_d = din("identb", [128, 128], BF16); identf_d = din("identf", [128, 128])
    onesb_d = din("onesb", [128, 128], BF16); onesf_d = din("onesf", [128, 128])

    y_p = dout("y_p", [TPC, D]); y_s = dout("y_s", [32, D])
    ak_p = dout("ak_p", [L, NTOK, 128]); av_p = dout("av_p", [L, NTOK, 128])
    bk_p = dout("bk_p", [L, 2, 512, 64]); bv_p = dout("bv_p", [L, 2, 512, 64])
    ak_s = dout("ak_s", [L, 256, 128]); av_s = dout("av_s", [L, 256, 128])
    bk_s = dout("bk_s", [L, 256, 64]); bv_s = dout("bv_s", [L, 256, 64])

    DBG = cfg.get("DBG", False)
    if DBG:
        dbg_o = dout("dbg_o", [192, NALL], BF16)
        dbg_xt = dout("dbg_xt", [128, 4096], F32); dbg_mT = dout("dbg_mT", [128, 4096], BF16)
        dbg_gT = dout("dbg_gT", [128, 8192], BF16); dbg_oaT = dout("dbg_oaT", [128, 4096], BF16)
        dbg_obT = dout("dbg_obT", [64, 4096], BF16)
    ag_h_in = dint("ag_h_in", [D, MYT], BF16); ag_h_out = dint("ag_h_out", [NC * D, MYT], BF16)
    ag_o_in = dint("ag_o_in", [192, NALL], BF16); ag_o_out = dint("ag_o_out", [NC * 192, NALL], BF16)
    x1 = dint("x1", [MYT, D], F32).ap()
    wg_s = dint("wg_s", [L, 128, 8, 2048], BF16).ap(); wa_s = dint("wa_s", [L, 128, 8, 1024], BF16).ap()
    wb_s = dint("wb_s", [L, 64, 8, 1024], BF16).ap(); wo_s = dint("wo_s", [L, 128, 8, 1024], BF16).ap()
    w1_s = dint("w1_s", [L, 128, 8, 4096], BF16).ap(); w2_s = dint("w2_s", [L, 128, 32, 1024], BF16).ap()
    ag_h_in_v = ag_h_in.ap().rearrange("(j p) t -> p j t", p=128)
    ag_h_out_v = ag_h_out.ap().rearrange("(r j p) t -> p r j t", r=NC, j=8, p=128)
    ag_o_in_a = ag_o_in.ap()
    ag_o_out_v = ag_o_out.ap().rearrange("(h e) t -> e h t", e=192)

    P = Prog(nc)
    pb = [nc.alloc_psum_tensor(f"pb{i}", [128, 512], F32) for i in range(8)]
    PB = lambda i: ("pb", i)

    A = SBAlloc(nc)
    identb = A("identb", [128, 128], BF16); identf = A("identf", [128, 128], F32)
    onesb = A("onesb", [128, 128], BF16); onesf = A("onesf", [128, 128], F32)
    biasA = A("biasA", [128, NKT + 3], F32); cblk = A("cblk", [128, 7 * 128], BF16)
    g1t = A("g1t", [128, D], F32); g2t = A("g2t", [128, D], F32)
    qkgt = A("qkgt", [128, 512], F32); lamt = A("lamt", [128, 256], F32)
    small = A("small", [128, 32], F32)
    bscr = A("bscr", [128, 8], F32)
    ARENA0 = A.off

    for t, d in ((identb, identb_d), (identf, identf_d), (onesb, onesb_d), (onesf, onesf_d),
                 (biasA, biasA_d), (cblk, cblk_d)):
        P.dma("sp", t[:], d, w=[t.name])
    CONST_R = [identb.name, identf.name, onesb.name, onesf.name, biasA.name, cblk.name]

    cast_rr = [0]

    def cast_copy(out, in_, r, w):
        k = cast_rr[0] % 3
        cast_rr[0] += 1
        if k == 0:
            P.dve(lambda e: e.tensor_copy(out=out, in_=in_), r=r, w=w)
        elif k == 1:
            P.act(lambda e: e.copy(out=out, in_=in_), r=r, w=w)
        else:
            P.pool(lambda e: e.tensor_copy(out=out, in_=in_), r=r, w=w)

    A = SBAlloc(nc, ARENA0)
    stg32 = [A(f"stg32_{i}", [128, 2048], F32) for i in range(2)]
    stg16 = [A(f"stg16_{i}", [128, 2048], BF16) for i in range(2)]
    pcount = [0]

    def prep(src_v, dst_v, np_, Adim, C):
        if C >= 2048:
            astep, cstep = 1, 2048
        else:
            astep, cstep = 2048 // C, C
        for a0 in range(0, Adim, astep):
            for c0 in range(0, C, cstep):
                i = pcount[0] % 2
                pcount[0] += 1
                s32 = stg32[i][0:np_, :].rearrange("p (a c) -> p a c", a=astep)
                s16 = stg16[i][0:np_, :].rearrange("p (a c) -> p a c", a=astep)
                P.dma("sp", s32, src_v[:, a0:a0 + astep, c0:c0 + cstep], w=[("stg32", i)])
                cast_copy(s16, s32, r=[("stg32", i)], w=[("stg16", i)])
                P.dma("pool", dst_v[:, a0:a0 + astep, c0:c0 + cstep], s16, r=[("stg16", i)])

    for l in range(L):
        prep(wgate[l].rearrange("(j p) c -> p j c", p=128), wg_s[l], 128, 8, 2048)
        prep(wbra[l].rearrange("(h e) c -> e h c", e=128), wa_s[l], 128, 8, 1024)
        prep(wbrb[l].rearrange("(h e) c -> e h c", e=64), wb_s[l], 64, 8, 1024)
        prep(wout[l].rearrange("(j p) c -> p j c", p=128), wo_s[l], 128, 8, 1024)
        prep(wff1[l].rearrange("(j p) c -> p j c", p=128), w1_s[l], 128, 8, 4096)
        prep(wff2[l].rearrange("(j p) c -> p j c", p=128), w2_s[l], 128, 32, 1024)
    P.barrier(bscr[:])
    STOP = cfg.get("STOP", "")
    if STOP == "prep":
        P.emit()
        return nc

    A = SBAlloc(nc, ARENA0)
    xt = A("xt", [128, 4, D], F32)
    hT = A("hT", [128, 8, 512], BF16)
    oaT = A("oaT", [128, 8, 512], BF16)
    obT = A("obT", [64, 8, 512], BF16)
    gT = A("gT", [128, 16, 512], BF16)
    mT = A("mT", [128, 8, 512], BF16)
    h2T = A("h2T", [128, 8, 512], BF16)
    u2T = A("u2T", [128, 32, 512], BF16)
    wring = [A(f"wring{i}", [128, 16384], BF16) for i in range(2)]
    hb32 = A("hb32", [128, D], F32)
    junk = A("junk", [128, D], BF16)
    tA = A("tA2", [128, 512], F32); tB = A("tB2", [128, 512], F32)
    rbuf = [A(f"rbuf{i}", [128, 512], BF16) for i in range(2)]
    P2_END = A.off

    bankrr = [0]

    def nbank(lo=0, hi=8):
        b = lo + bankrr[0] % (hi - lo)
        bankrr[0] += 1
        return b

    def norm_transpose(xsrc, n, gt, dst, dkey, xkey):
        ss = small[0:n, 2:3]
        rs = small[0:n, 3:4]
        P.act(lambda e: e.activation(out=junk[0:n, :], in_=xsrc, func=AF.Square, scale=1.0 / math.sqrt(D),
                                     accum_out=ss), r=[xkey], w=["junk", "nt_ss"])
        P.dve(lambda e: e.tensor_scalar(out=rs, in0=ss, scalar1=1.0, scalar2=EPS, op0=ALU.mult, op1=ALU.add),
              r=["nt_ss"], w=["nt_rs"])
        P.act(lambda e: e.activation(out=rs, in_=rs, func=AF.Sqrt), w=["nt_rs"])
        P.dve(lambda e: e.reciprocal(out=rs, in_=rs), w=["nt_rs"])
        P.dve(lambda e: e.scalar_tensor_tensor(out=hb32[0:n, :], in0=xsrc, scalar=rs, in1=gt[0:n, :],
                                               op0=ALU.mult, op1=ALU.mult),
              r=[xkey, "nt_rs", gt.name], w=["hb32"])
        for half in range(2):
            bk = 4 + half
            for jj in range(4):
                j = half * 4 + jj
                P.pe(lambda e, j=j, jj=jj, bk=bk: e.transpose(out=pb[bk][:, jj * 128:jj * 128 + n],
                                                             in_=hb32[0:n, j * 128:(j + 1) * 128],
                                                             identity=identf[0:n, 0:n]),
                     r=["hb32", identf.name], w=[PB(bk)])
            src = pb[bk][:, :].rearrange("p (j t) -> p j t", j=4)[:, :, 0:n]
            d = dst[:, half * 4:half * 4 + 4, :]
            if half == 0:
                P.act(lambda e, d=d, src=src: e.copy(out=d, in_=src), w=[PB(bk), dkey])
            else:
                P.dve(lambda e, d=d, src=src: e.tensor_copy(out=d, in_=src), w=[PB(bk), dkey])

    def my_tiles():
        tl = [(i * 512, 512) for i in range(TPC // 512)]
        tl.append((TPC, 32))
        return tl

    def load_x(l, c0, N):
        src = (xp if c0 < TPC else xs) if l == 0 else None
        nsub = max(1, N // 128)
        n = min(N, 128)
        if l == 0:
            if c0 < TPC:
                P.dma("sp", xt[:, 0:nsub, :], xp[c0:c0 + N, :].rearrange("(s p) d -> p s d", p=128), w=["xt"])
            else:
                P.dma("sp", xt[0:n, 0, :], xs[:, :], w=["xt"])
        else:
            if N >= 128:
                P.dma("sp", xt[:, 0:nsub, :], x1[c0:c0 + N, :].rearrange("(s p) d -> p s d", p=128),
                      r=["x1"], w=["xt"])
            else:
                P.dma("sp", xt[0:n, 0, :], x1[c0:c0 + N, :], r=["x1"], w=["xt"])
        return nsub, n

    def emit_hT(l_next, c0, N, nsub, n):
        for s in range(nsub):
            norm_transpose(xt[0:n, s, :], n, g1t, hT[:, :, s * 128:s * 128 + n], "hT", "xt")
        P.dma("pool", ag_h_in_v[:, :, c0:c0 + N], hT[:, :, 0:N], r=["hT"], w=["ag_h_in"])

    def allgather(src, dst, rk, wk):
        P.add("pool", lambda e: e.collective_compute("AllGather", ALU.bypass, replica_groups=[list(range(NC))],
                                                     ins=[src.ap().opt()], outs=[dst.ap().opt()]),
              r=[rk], w=[wk], kind="cc")

    P.dma("sp", g1t[:], g1b[0], w=[g1t.name])
    for (c0, N) in my_tiles():
        nsub, n = load_x(0, c0, N)
        emit_hT(0, c0, N, nsub, n)
    if STOP == "p0":
        P.emit()
        return nc
    allgather(ag_h_in, ag_h_out, "ag_h_in", "ag_h_out")
    if STOP == "ag1":
        P.emit()
        return nc

    def p1_layout():
        A = SBAlloc(nc, ARENA0)
        o = {}
        o["KaT"] = [A(f"KaT{i}", [67, S], BF16) for i in range(2)]
        o["Va"] = A("Va", [128, NKT, 128], BF16)
        o["hTs"] = [A(f"hTs{i}", [128, 8, 512], BF16) for i in range(2)]
        o["wc"] = A("wc", [128, 8, 576], BF16)
        o["bblkt"] = A("bblkt", [128, 11 * 128], F32)
        o["bblkst"] = A("bblkst", [128, 80], F32)
        o["QaT"] = [A(f"QaT{i}", [67, 512], BF16) for i in range(2)]
        o["QbT"] = A("QbT", [64, 512], BF16)
        o["KbT"] = A("KbT", [64, 1024], BF16)
        o["Vb"] = A("Vb", [128, 8, 64], BF16)
        o["zs"] = [A(f"zs{i}", [128, 576], F32) for i in range(2)]
        o["sq"] = [A(f"sq{i}", [128, 512], F32) for i in range(2)]
        o["kn32"] = [A(f"kn32{i}", [128, 512], F32) for i in range(2)]
        o["Pa"] = [[A(f"Pa{m}_{i}", [128, 512], BF16) for i in range(2)] for m in range(2)]
        o["Pb"] = [A(f"Pb{i}", [128, 512], BF16) for i in range(2)]
        o["sb32"] = A("sb32", [128, 512], F32)
        o["fA"] = A("fA", [128, 512], F32); o["fB"] = A("fB", [128, 512], F32)
        o["obf"] = A("obf", [128, 512], BF16)
        o["KsT"] = [A(f"KsT{i}", [67, (NCT + 1) * 128], BF16) for i in range(2)]
        o["Vs"] = A("Vs", [128, NCT + 1, 128], BF16)
        o["ck"] = A("ck", [128, 8, 128], F32)
        o["cv"] = A("cv", [128, 8, 128], F32)
        o["wst"] = A("wst", [128, 576], F32)
        o["QsT"] = [A(f"QsT{i}", [67, 256], BF16) for i in range(2)]
        o["KnT"] = [A(f"KnT{i}", [64, 256], BF16) for i in range(2)]
        o["QbsT"] = A("QbsT", [64, 256], BF16)
        o["KbnT"] = A("KbnT", [64, 256], BF16)
        o["KbsT"] = A("KbsT", [64, 640], BF16)
        o["Vbs"] = A("Vbs", [128, 5, 64], BF16)
        o["cbkt"] = A("cbkt", [128, 4, 64], F32); o["cbvt"] = A("cbvt", [128, 4, 64], F32)
        o["vas"] = [A(f"vas{i}", [128, 128], BF16) for i in range(2)]
        o["vbs"] = [A(f"vbs{i}", [128, 64], BF16) for i in range(2)]
        o["Vsn"] = A("Vsn", [16, 128], BF16); o["Vbsn"] = A("Vbsn", [16, 64], BF16)
        return o

    T1 = p1_layout()

    def p1(l):
        lam_init = 0.8 - 0.6 * math.exp(-0.3 * l)
        KaT, Va, hTs, wc = T1["KaT"], T1["Va"], T1["hTs"], T1["wc"]
        QaT, QbT, KbT, Vb = T1["QaT"], T1["QbT"], T1["KbT"], T1["Vb"]
        zsL, sqL, kn32L = T1["zs"], T1["sq"], T1["kn32"]
        sq = sqL[0]
        Pa, Pbb, sb32, fA, fB, obf = T1["Pa"], T1["Pb"], T1["sb32"], T1["fA"], T1["fB"], T1["obf"]
        bblkt, bblkst = T1["bblkt"], T1["bblkst"]
        for j in range(8):
            P.dma("sp", T1["wst"][:], win[l, j * 128:(j + 1) * 128, :], w=["wst"])
            cast_copy(wc[:, j, :], T1["wst"][:], r=["wst"], w=["wc"])
        P.dma("sp", qkgt[:], qkg[l], w=["qkgt"])
        P.dma("sp", lamt[:], lamv[l], w=["lamt"])
        P.dma("sp", small[:, 1:2], sublng[l], w=["gcol"])
        P.dma("sp", bblkt[:], bblk[l], w=["bblkt"])
        P.dma("sp", bblkst[:], bblks[l], w=["bblkst"])
        for m in range(2):
            P.dma("sp", KaT[m][64:67, :], kaugS, w=[("KaT", m)])
            P.dma("sp", QaT[m][64:67, :], qaug, w=[("QaT", m)])
            P.dma("sp", T1["KsT"][m][64:67, :], kaugS[:, 0:(NCT + 1) * 128], w=[("KsT", m)])
            P.dma("sp", T1["QsT"][m][64:67, :], qaugs, w=[("QsT", m)])
        P.dve(lambda e: e.tensor_tensor(out=sq[:, 0:64], in0=lamt[:, 0:64], in1=lamt[:, 64:128], op=ALU.mult),
              r=["lamt"], w=[("sq", 0)])
        P.dve(lambda e: e.tensor_tensor(out=sq[:, 64:128], in0=lamt[:, 128:192], in1=lamt[:, 192:256], op=ALU.mult),
              r=["lamt"], w=[("sq", 0)])
        P.dve(lambda e: e.tensor_reduce(out=small[:, 4:6], in_=sq[:, 0:128].rearrange("p (a b) -> p a b", a=2),
                                        axis=AX.X, op=ALU.add), r=[("sq", 0)], w=["lam_s"])
        P.act(lambda e: e.activation(out=small[:, 6:8], in_=small[:, 4:6], func=AF.Exp), r=["lam_s"], w=["lam_e"])
        P.dve(lambda e: e.tensor_tensor(out=small[:, 0:1], in0=small[:, 7:8], in1=small[:, 6:7], op=ALU.subtract),
              r=["lam_e"], w=["neglam"])
        P.dve(lambda e: e.tensor_scalar(out=small[:, 0:1], in0=small[:, 0:1], scalar1=1.0, scalar2=-lam_init,
                                        op0=ALU.mult, op1=ALU.add), w=["neglam"])
        P.dve(lambda e: e.tensor_scalar(out=small[:, 1:2], in0=small[:, 1:2], scalar1=1.0 - lam_init, scalar2=0.0,
                                        op0=ALU.mult, op1=ALU.add), w=["gcol"])
        neglam = small[:, 0:1]
        gcol = small[:, 1:2]

        ZB = ((6, 7), (4, 5))

        def inproj_mm(hsl, tcol, p):
            b0, b1 = ZB[p]
            for j in range(8):
                P.pe(lambda e, j=j: e.matmul(pb[b0][:, :], lhsT=hTs[hsl][:, j, tcol:tcol + 128], rhs=wc[:, j, 0:512],
                                             start=(j == 0), stop=(j == 7)), r=[("hTs", hsl), "wc"], w=[PB(b0)])
            for j in range(8):
                P.pe(lambda e, j=j: e.matmul(pb[b1][:, 0:64], lhsT=hTs[hsl][:, j, tcol:tcol + 128],
                                             rhs=wc[:, j, 512:576], start=(j == 0), stop=(j == 7)),
                     r=[("hTs", hsl), "wc"], w=[PB(b1)])

        def inproj_chain(p):
            b0, b1 = ZB[p]
            zs, sq, kn32 = zsL[p], sqL[p], kn32L[p]
            kz, ksq, kkn, kss = ("zs", p), ("sq", p), ("kn32", p), ("ss8", p)
            P.act(lambda e: e.copy(out=zs[:, 0:512], in_=pb[b0][:, :]), w=[PB(b0), kz])
            P.act(lambda e: e.copy(out=zs[:, 512:576], in_=pb[b1][:, 0:64]), w=[PB(b1), kz])
            P.dve(lambda e: e.tensor_tensor(out=sq[:, :], in0=zs[:, 0:512], in1=zs[:, 0:512], op=ALU.mult),
                  r=[kz], w=[ksq])
            ss8 = small[:, 8 + 8 * p:16 + 8 * p]
            P.dve(lambda e: e.tensor_reduce(out=ss8, in_=sq[:, :].rearrange("p (a b) -> p a b", a=8), axis=AX.X,
                                            op=ALU.add), r=[ksq], w=[kss])
            P.dve(lambda e: e.tensor_scalar(out=ss8, in0=ss8, scalar1=1.0 / 64, scalar2=EPS, op0=ALU.mult,
                                            op1=ALU.add), w=[kss])
            P.act(lambda e: e.activation(out=ss8, in_=ss8, func=AF.Sqrt), w=[kss])
            P.dve(lambda e: e.reciprocal(out=ss8, in_=ss8), w=[kss])
            P.dve(lambda e: e.tensor_tensor(out=kn32[:, :].rearrange("p (a b) -> p a b", a=8),
                                            in0=zs[:, 0:512].rearrange("p (a b) -> p a b", a=8),
                                            in1=ss8.rearrange("p (a o) -> p a o", o=1).to_broadcast([128, 8, 64]),
                                            op=ALU.mult), r=[kz, kss], w=[kkn])
            P.pool(lambda e: e.tensor_tensor(out=kn32[:, :], in0=kn32[:, :], in1=qkgt[:, :], op=ALU.mult),
                   r=["qkgt"], w=[kkn])
            return zs, kn32, kz, kkn

        def transposes(dsts, kn32, kkn):
            grp = [0, 1, 2, 3, 6, 7]
            for idx in range(6):
                g = grp[idx]
                bk = 0 if idx < 4 else 1
                cc = (idx % 4) * 128
                P.pe(lambda e, g=g, bk=bk, cc=cc: e.transpose(out=pb[bk][0:64, cc:cc + 128],
                                                             in_=kn32[:, g * 64:(g + 1) * 64], identity=identf[:, :]),
                     r=[kkn, identf.name], w=[PB(bk)])
            for idx in range(6):
                bk = 0 if idx < 4 else 1
                cc = (idx % 4) * 128
                dst, key = dsts[idx]
                if idx % 2 == 0:
                    P.act(lambda e, dst=dst, bk=bk, cc=cc: e.copy(out=dst, in_=pb[bk][0:64, cc:cc + 128]),
                          w=[PB(bk), key])
                else:
                    P.dve(lambda e, dst=dst, bk=bk, cc=cc: e.tensor_copy(out=dst, in_=pb[bk][0:64, cc:cc + 128]),
                          w=[PB(bk), key])

        def finalize_A(N, col0):
            P.dve(lambda e: e.reciprocal(out=fA[:, 0:N], in_=pb[4][:, 0:N]), w=[PB(4), "fA"])
            P.dve(lambda e: e.tensor_tensor(out=fA[:, 0:N], in0=pb[2][:, 0:N], in1=fA[:, 0:N], op=ALU.mult),
                  w=[PB(2), "fA"])
            P.dve(lambda e: e.reciprocal(out=fB[:, 0:N], in_=pb[5][:, 0:N]), w=[PB(5), "fB"])
            P.dve(lambda e: e.tensor_tensor(out=fB[:, 0:N], in0=pb[3][:, 0:N], in1=fB[:, 0:N], op=ALU.mult),
                  w=[PB(3), "fB"])
            P.dve(lambda e: e.scalar_tensor_tensor(out=fA[:, 0:N], in0=fB[:, 0:N], scalar=neglam, in1=fA[:, 0:N],
                                                   op0=ALU.mult, op1=ALU.add), r=["neglam"], w=["fA", "fB"])
            P.act(lambda e: e.activation(out=fB[:, 0:N], in_=fA[:, 0:N], func=AF.Square), r=["fA"], w=["fB"])
            P.pe(lambda e: e.matmul(pb[0][:, 0:N], lhsT=onesf[:, :], rhs=fB[:, 0:N], start=True, stop=True),
                 r=["fB", onesf.name], w=[PB(0)])
            P.dve(lambda e: e.tensor_scalar(out=fB[:, 0:N], in0=pb[0][:, 0:N], scalar1=1.0 / 128, scalar2=EPS,
                                            op0=ALU.mult, op1=ALU.add), w=[PB(0), "fB"])
            P.act(lambda e: e.activation(out=fB[:, 0:N], in_=fB[:, 0:N], func=AF.Sqrt), w=["fB"])
            P.dve(lambda e: e.reciprocal(out=fB[:, 0:N], in_=fB[:, 0:N]), w=["fB"])
            P.dve(lambda e: e.scalar_tensor_tensor(out=obf[:, 0:N], in0=fA[:, 0:N], scalar=gcol, in1=fB[:, 0:N],
                                                   op0=ALU.mult, op1=ALU.mult), r=["fA", "fB", "gcol"], w=["obf"])
            P.dma("pool", ag_o_in_a[0:128, col0:col0 + N], obf[:, 0:N], r=["obf"], w=["ag_o_in"])

        def finalize_B(N, col0, bo, bl):
            P.dve(lambda e: e.reciprocal(out=fA[0:64, 0:N], in_=pb[bl][0:64, 0:N]), w=[PB(bl), "fA"])
            P.dve(lambda e: e.tensor_tensor(out=obf[0:64, 0:N], in0=pb[bo][0:64, 0:N], in1=fA[0:64, 0:N],
                                            op=ALU.mult), r=["fA"], w=[PB(bo), "obf"])
            P.dma("pool", ag_o_in_a[128:192, col0:col0 + N], obf[0:64, 0:N], r=["obf"], w=["ag_o_in"])

        for st in range(NST):
            b, i = st // STB, st % STB
            hsl = st % 2
            rk, c0 = st // TPR, (st % TPR) * 512
            P.dma("sp", hTs[hsl][:, :, :], ag_h_out_v[:, rk, :, c0:c0 + 512], r=["ag_h_out"], w=[("hTs", hsl)])
            half = i % 2
            inproj_mm(hsl, 0, 0)
            for s in range(4):
                kt = i * 4 + s
                tok0 = st * 512 + s * 128
                if s + 1 < 4:
                    inproj_mm(hsl, (s + 1) * 128, (s + 1) % 2)
                zs, kn32, kz, kkn = inproj_chain(s % 2)
                P.dma("pool", ak_p[l, tok0:tok0 + 128, :], kn32[:, 128:256], r=[kkn])
                P.dma("pool", av_p[l, tok0:tok0 + 128, :], zs[:, 256:384], r=[kz])
                if i == STB - 1:
                    P.dma("pool", bk_p[l, b, s * 128:(s + 1) * 128, :], kn32[:, 448:512], r=[kkn])
                    P.dma("pool", bv_p[l, b, s * 128:(s + 1) * 128, :], zs[:, 512:576], r=[kz])
                transposes([
                    (QaT[0][0:64, s * 128:(s + 1) * 128], ("QaT", 0)),
                    (QaT[1][0:64, s * 128:(s + 1) * 128], ("QaT", 1)),
                    (KaT[0][0:64, kt * 128:(kt + 1) * 128], ("KaT", 0)),
                    (KaT[1][0:64, kt * 128:(kt + 1) * 128], ("KaT", 1)),
                    (QbT[0:64, s * 128:(s + 1) * 128], "QbT"),
                    (KbT[0:64, (half * 4 + s) * 128:(half * 4 + s + 1) * 128], "KbT"),
                ], kn32, kkn)
                P.pool(lambda e, kt=kt: e.tensor_copy(out=Va[:, kt, :], in_=zs[:, 256:384]), r=[kz], w=["Va"])
                P.pool(lambda e, s=s: e.tensor_copy(out=Vb[:, half * 4 + s, :], in_=zs[:, 512:576]),
                       r=[kz], w=["Vb"])
            tl = list(range(4, 8)) if i == 0 else list(range(8))
            for n_, t in enumerate(tl):
                rt = ((1 - half) * 4 + t) if t < 4 else (half * 4 + t - 4)
                bk = n_ % 2
                P.pe(lambda e, rt=rt, bk=bk: e.matmul(pb[bk][:, :], lhsT=KbT[0:64, rt * 128:(rt + 1) * 128],
                                                     rhs=QbT[0:64, :], start=True, stop=True),
                     r=["KbT", "QbT"], w=[PB(bk)])
                P.dve(lambda e, t=t, bk=bk: e.scalar_tensor_tensor(out=sb32[:, :], in0=pb[bk][:, :], scalar=0.125,
                                                                   in1=bblkt[:, (7 - t) * 128:(7 - t) * 128 + 512],
                                                                   op0=ALU.mult, op1=ALU.add),
                      r=["bblkt"], w=[PB(bk), "sb32"])
                P.act(lambda e, bk=bk: e.activation(out=Pbb[bk][:, :], in_=sb32[:, :], func=AF.Exp),
                      r=["sb32"], w=[("Pb", bk)])
                P.pe(lambda e, rt=rt, bk=bk, n_=n_: e.matmul(pb[2][0:64, :], lhsT=Vb[:, rt, :], rhs=Pbb[bk][:, :],
                                                            start=(n_ == 0), stop=(n_ == len(tl) - 1)),
                     r=["Vb", ("Pb", bk)], w=[PB(2)])
                P.pe(lambda e, bk=bk, n_=n_: e.matmul(pb[3][0:64, :], lhsT=onesb[:, 0:64], rhs=Pbb[bk][:, :],
                                                     start=(n_ == 0), stop=(n_ == len(tl) - 1)),
                     r=[onesb.name, ("Pb", bk)], w=[PB(3)])
            finalize_B(512, st * 512, 2, 3)
            nkt = 4 * (i + 1)

            def qk(kt):
                j = kt - 4 * i
                for m in range(2):
                    P.pe(lambda e, m=m, kt=kt, j=j: e.matmul(pb[m][:, :], lhsT=KaT[m][0:67, kt * 128:(kt + 1) * 128],
                                                            rhs=QaT[m][0:67, :], start=True, stop=(j < 0)),
                         r=[("KaT", m), ("QaT", m)], w=[PB(m)])
                    if j >= 0:
                        P.pe(lambda e, m=m, j=j: e.matmul(pb[m][:, :], lhsT=identb[:, :],
                                                         rhs=cblk[:, (3 - j) * 128:(3 - j) * 128 + 512],
                                                         start=False, stop=True),
                             r=[identb.name, cblk.name], w=[PB(m)])

            qk(0)
            for kt in range(nkt):
                sl = kt % 2
                mm = 4 * i - kt
                for m in range(2):
                    P.act(lambda e, m=m, sl=sl, mm=mm: e.activation(out=Pa[m][sl][:, :], in_=pb[m][:, :], func=AF.Exp,
                                                                   bias=biasA[:, mm + 3:mm + 4], scale=0.125),
                          r=[biasA.name], w=[PB(m), ("Pa", m, sl)])
                if kt + 1 < nkt:
                    qk(kt + 1)
                for m in range(2):
                    P.pe(lambda e, m=m, sl=sl, kt=kt: e.matmul(pb[2 + m][:, :], lhsT=Va[:, kt, :], rhs=Pa[m][sl][:, :],
                                                              start=(kt == 0), stop=(kt == nkt - 1)),
                         r=["Va", ("Pa", m, sl)], w=[PB(2 + m)])
                    P.pe(lambda e, m=m, sl=sl, kt=kt: e.matmul(pb[4 + m][:, :], lhsT=onesb[:, :], rhs=Pa[m][sl][:, :],
                                                              start=(kt == 0), stop=(kt == nkt - 1)),
                         r=[onesb.name, ("Pa", m, sl)], w=[PB(4 + m)])
            finalize_A(512, st * 512)

        KsT, Vs, ck, cv = T1["KsT"], T1["Vs"], T1["ck"], T1["cv"]
        QsT, KnT, QbsT, KbnT, KbsT, Vbs = T1["QsT"], T1["KnT"], T1["QbsT"], T1["KbnT"], T1["KbsT"], T1["Vbs"]
        cbkt, cbvt, vas, vbs, Vsn, Vbsn = T1["cbkt"], T1["cbvt"], T1["vas"], T1["vbs"], T1["Vsn"], T1["Vbsn"]
        for u in range(2):
            hsl = u
            for rr in range(4):
                P.dma("sp", hTs[hsl][:, :, rr * 32:(rr + 1) * 32], ag_h_out_v[:, 4 * u + rr, :, TPC:TPC + 32],
                      r=["ag_h_out"], w=[("hTs", hsl)])
            inproj_mm(hsl, 0, u)
            zs, kn32, kz, kkn = inproj_chain(u)
            ts0 = u * 128
            P.dma("pool", ak_s[l, ts0:ts0 + 128, :], kn32[:, 128:256], r=[kkn])
            P.dma("pool", av_s[l, ts0:ts0 + 128, :], zs[:, 256:384], r=[kz])
            P.dma("pool", bk_s[l, ts0:ts0 + 128, :], kn32[:, 448:512], r=[kkn])
            P.dma("pool", bv_s[l, ts0:ts0 + 128, :], zs[:, 512:576], r=[kz])
            transposes([
                (QsT[0][0:64, ts0:ts0 + 128], ("QsT", 0)), (QsT[1][0:64, ts0:ts0 + 128], ("QsT", 1)),
                (KnT[0][0:64, ts0:ts0 + 128], ("KnT", 0)), (KnT[1][0:64, ts0:ts0 + 128], ("KnT", 1)),
                (QbsT[0:64, ts0:ts0 + 128], "QbsT"), (KbnT[0:64, ts0:ts0 + 128], "KbnT"),
            ], kn32, kkn)
            P.pool(lambda e, u=u: e.tensor_copy(out=vas[u][:, :], in_=zs[:, 256:384]), r=[kz], w=[("vas", u)])
            P.pool(lambda e, u=u: e.tensor_copy(out=vbs[u][:, :], in_=zs[:, 512:576]), r=[kz], w=[("vbs", u)])
        for bs in range(16):
            u, o16 = bs // 8, (bs % 8) * 16
            q0c = bs * 16
            for hf in range(max(1, NCT // 8)):
                nt8 = min(8, NCT)
                P.dma("sp", ck[:, 0:nt8, :], cak[l, bs, hf * 1024:hf * 1024 + nt8 * 128, :].rearrange(
                    "(t p) d -> p t d", p=128), w=["ck"])
                P.dma("sp", cv[:, 0:nt8, :], cav[l, bs, hf * 1024:hf * 1024 + nt8 * 128, :].rearrange(
                    "(t p) d -> p t d", p=128), w=["cv"])
                for t4 in range(nt8 // 4):
                    for m in range(2):
                        for tt in range(4):
                            t = t4 * 4 + tt
                            P.pe(lambda e, m=m, t=t, tt=tt: e.transpose(out=pb[m][0:64, tt * 128:(tt + 1) * 128],
                                                                       in_=ck[:, t, m * 64:(m + 1) * 64],
                                                                       identity=identf[:, :]),
                                 r=["ck", identf.name], w=[PB(m)])
                        kc = (hf * 8 + t4 * 4) * 128
                        if m == 0:
                            P.act(lambda e, kc=kc: e.copy(out=KsT[0][0:64, kc:kc + 512], in_=pb[0][0:64, :]),
                                  w=[PB(0), ("KsT", 0)])
                        else:
                            P.dve(lambda e, kc=kc: e.tensor_copy(out=KsT[1][0:64, kc:kc + 512], in_=pb[1][0:64, :]),
                                  w=[PB(1), ("KsT", 1)])
                P.pool(lambda e, hf=hf, nt8=nt8: e.tensor_copy(out=Vs[:, hf * 8:hf * 8 + nt8, :], in_=cv[:, 0:nt8, :]),
                       r=["cv"], w=["Vs"])
            kn0 = NCT * 128
            P.act(lambda e, q0c=q0c: e.copy(out=KsT[0][0:64, kn0:kn0 + 16], in_=KnT[0][0:64, q0c:q0c + 16]),
                  r=[("KnT", 0)], w=[("KsT", 0)])
            P.dve(lambda e, q0c=q0c: e.tensor_copy(out=KsT[1][0:64, kn0:kn0 + 16], in_=KnT[1][0:64, q0c:q0c + 16]),
                  r=[("KnT", 1)], w=[("KsT", 1)])
            P.dma("sp", Vsn[:, :], vas[u][o16:o16 + 16, :], r=[("vas", u)], w=["Vsn"])
            for kt in range(NCT + 1):
                last = kt == NCT
                nk = 16 if last else 128
                sl = kt % 2
                for m in range(2):
                    P.pe(lambda e, m=m, kt=kt, nk=nk, last=last: e.matmul(
                        pb[m][0:nk, 0:16], lhsT=KsT[m][0:67, kt * 128:kt * 128 + nk], rhs=QsT[m][0:67, q0c:q0c + 16],
                        start=True, stop=not last), r=[("KsT", m), ("QsT", m)], w=[PB(m)])
                    if last:
                        P.pe(lambda e, m=m: e.matmul(pb[m][0:16, 0:16], lhsT=identb[0:16, 0:16],
                                                     rhs=cblk[0:16, 3 * 128:3 * 128 + 16], start=False, stop=True),
                             r=[identb.name, cblk.name], w=[PB(m)])
                mm = 0 if last else NCT - kt
                for m in range(2):
                    P.act(lambda e, m=m, sl=sl, mm=mm, nk=nk: e.activation(
                        out=Pa[m][sl][0:nk, 0:16], in_=pb[m][0:nk, 0:16], func=AF.Exp,
                        bias=biasA[0:nk, mm + 3:mm + 4], scale=0.125), r=[biasA.name], w=[PB(m), ("Pa", m, sl)])
                for m in range(2):
                    vl = Vsn[0:16, :] if last else Vs[:, kt, :]
                    P.pe(lambda e, m=m, sl=sl, kt=kt, nk=nk, vl=vl: e.matmul(
                        pb[2 + m][:, q0c:q0c + 16], lhsT=vl, rhs=Pa[m][sl][0:nk, 0:16],
                        start=(kt == 0), stop=last), r=["Vs", "Vsn", ("Pa", m, sl)], w=[PB(2 + m)])
                    P.pe(lambda e, m=m, sl=sl, kt=kt, nk=nk: e.matmul(
                        pb[4 + m][:, q0c:q0c + 16], lhsT=onesb[0:nk, :], rhs=Pa[m][sl][0:nk, 0:16],
                        start=(kt == 0), stop=last), r=[onesb.name, ("Pa", m, sl)], w=[PB(4 + m)])
        finalize_A(256, NTOK)
        for bs in range(16):
            u, o16 = bs // 8, (bs % 8) * 16
            q0c = bs * 16
            P.dma("sp", cbkt[:, :, :], cbk[l, bs].rearrange("(t p) d -> p t d", p=128), w=["cbkt"])
            P.dma("sp", cbvt[:, :, :], cbv[l, bs].rearrange("(t p) d -> p t d", p=128), w=["cbvt"])
            for tt in range(4):
                P.pe(lambda e, tt=tt: e.transpose(out=pb[0][0:64, tt * 128:(tt + 1) * 128], in_=cbkt[:, tt, :],
                                                  identity=identf[:, :]), r=["cbkt", identf.name], w=[PB(0)])
            P.act(lambda e: e.copy(out=KbsT[0:64, 0:512], in_=pb[0][0:64, :]), w=[PB(0), "KbsT"])
            P.dve(lambda e, q0c=q0c: e.tensor_copy(out=KbsT[0:64, 512:528], in_=KbnT[0:64, q0c:q0c + 16]),
                  r=["KbnT"], w=["KbsT"])
            P.pool(lambda e: e.tensor_copy(out=Vbs[:, 0:4, :], in_=cbvt[:, :, :]), r=["cbvt"], w=["Vbs"])
            P.dma("sp", Vbsn[:, :], vbs[u][o16:o16 + 16, :], r=[("vbs", u)], w=["Vbsn"])
            for t in range(5):
                last = t == 4
                nk = 16 if last else 128
                sl = t % 2
                P.pe(lambda e, t=t, nk=nk: e.matmul(pb[1][0:nk, 0:16], lhsT=KbsT[0:64, t * 128:t * 128 + nk],
                                                   rhs=QbsT[0:64, q0c:q0c + 16], start=True, stop=True),
                     r=["KbsT", "QbsT"], w=[PB(1)])
                P.dve(lambda e, t=t, nk=nk: e.scalar_tensor_tensor(out=sb32[0:nk, 0:16], in0=pb[1][0:nk, 0:16],
                                                                   scalar=0.125, in1=bblkst[0:nk, t * 16:(t + 1) * 16],
                                                                   op0=ALU.mult, op1=ALU.add),
                      r=["bblkst"], w=[PB(1), "sb32"])
                P.act(lambda e, sl=sl, nk=nk: e.activation(out=Pbb[sl][0:nk, 0:16], in_=sb32[0:nk, 0:16], func=AF.Exp),
                      r=["sb32"], w=[("Pb", sl)])
                vl = Vbsn[0:16, :] if last else Vbs[:, t, :]
                P.pe(lambda e, t=t, sl=sl, nk=nk, vl=vl: e.matmul(pb[6][0:64, q0c:q0c + 16], lhsT=vl,
                                                                 rhs=Pbb[sl][0:nk, 0:16], start=(t == 0), stop=last),
                     r=["Vbs", "Vbsn", ("Pb", sl)], w=[PB(6)])
                P.pe(lambda e, t=t, sl=sl, nk=nk: e.matmul(pb[7][0:64, q0c:q0c + 16], lhsT=onesb[0:nk, 0:64],
                                                          rhs=Pbb[sl][0:nk, 0:16], start=(t == 0), stop=last),
                     r=[onesb.name, ("Pb", sl)], w=[PB(7)])
        finalize_B(256, NTOK, 6, 7)

    def p2(l):
        last_layer = l == L - 1
        P.dma("sp", g2t[:], g2b[l], w=[g2t.name])
        if not last_layer:
            P.dma("sp", g1t[:], g1b[l + 1], w=[g1t.name])
        wslot = [0]

        def wload(src_v, np_, shape3):
            i = wslot[0] % 2
            wslot[0] += 1
            a, c = shape3
            v = wring[i][0:np_, 0:a * c].rearrange("p (a c) -> p a c", a=a)
            h = a // 2
            P.dma("sp", v[:, 0:h, :], src_v[:, 0:h, :], w=[("wr", i, 0)])
            P.dma("sp", v[:, h:a, :], src_v[:, h:a, :], w=[("wr", i, 1)])
            return v, ("wr", i)

        def dump():
            if DBG:
                P.dma("sp", dbg_xt, xt[:, :, :].rearrange("p a b -> p (a b)"), r=["xt"])
                P.dma("sp", dbg_mT, mT[:, :, :].rearrange("p a b -> p (a b)"), r=["mT"])
                P.dma("sp", dbg_gT, gT[:, :, :].rearrange("p a b -> p (a b)"), r=["gT"])
                P.dma("sp", dbg_oaT, oaT[:, :, :].rearrange("p a b -> p (a b)"), r=["oaT"])
                P.dma("sp", dbg_obT, obT[:, :, :].rearrange("p a b -> p (a b)"), r=["obT"])

        STILE = cfg.get("STILE", 0)
        for ti, (c0, N) in enumerate(my_tiles()):
            STOP = cfg.get("STOP", "") if ti == STILE else ""
            nsub, n = load_x(l, c0, N)
            P.dma("sp", hT[:, :, 0:N], ag_h_in_v[:, :, c0:c0 + N], r=["ag_h_in"], w=["hT"])
            if c0 < TPC:
                dyn = lambda e, c0=c0, N=N: bass.ds(P.pid * TPC + c0, N)
            else:
                dyn = lambda e, N=N: bass.ds(P.pid * 32 + NTOK, N)
            P.add("sp", lambda e, dyn=dyn, N=N: e.dma_start(out=oaT[:, :, 0:N], in_=ag_o_out_v[0:128, :, dyn(e)]),
                  r=["ag_o_out"], w=["oaT"], kind="d", late=True)
            P.add("sp", lambda e, dyn=dyn, N=N: e.dma_start(out=obT[:, :, 0:N], in_=ag_o_out_v[128:192, :, dyn(e)]),
                  r=["ag_o_out"], w=["obT"], kind="d", late=True)
            for half in range(2):
                wv, wk = wload(wg_s[l][:, :, half * 1024:(half + 1) * 1024], 128, (8, 1024))
                for gc in range(8):
                    bk = nbank(0, 4)
                    for j in range(8):
                        P.pe(lambda e, wv=wv, gc=gc, j=j, bk=bk: e.matmul(
                            pb[bk][:, 0:N], lhsT=wv[:, j, gc * 128:(gc + 1) * 128], rhs=hT[:, j, 0:N],
                            start=(j == 0), stop=(j == 7)), r=[wk + (0,), wk + (1,), "hT"], w=[PB(bk)])
                    P.act(lambda e, bk=bk, half=half, gc=gc: e.activation(out=gT[:, half * 8 + gc, 0:N],
                                                                        in_=pb[bk][:, 0:N], func=AF.Sigmoid),
                          w=[PB(bk), "gT"])
            if STOP == "p2b":
                return dump()
            wav, wak = wload(wa_s[l], 128, (8, 1024))
            wbv, wbk = wload(wb_s[l], 64, (8, 1024))
            for oc in range(8):
                ba, bb = nbank(4, 8), nbank(4, 8)
                for h in range(8):
                    P.pe(lambda e, h=h, oc=oc, ba=ba: e.matmul(pb[ba][:, 0:N], lhsT=wav[:, h, oc * 128:(oc + 1) * 128],
                                                              rhs=oaT[:, h, 0:N], start=(h == 0), stop=(h == 7)),
                         r=[wak + (0,), wak + (1,), "oaT"], w=[PB(ba)])
                for h in range(8):
                    P.pe(lambda e, h=h, oc=oc, bb=bb: e.matmul(pb[bb][:, 0:N], lhsT=wbv[0:64, h, oc * 128:(oc + 1) * 128],
                                                              rhs=obT[0:64, h, 0:N], start=(h == 0), stop=(h == 7)),
                         r=[wbk + (0,), wbk + (1,), "obT"], w=[PB(bb)])
                P.dve(lambda e, oc=oc, ba=ba: e.tensor_tensor(out=tA[:, 0:N], in0=pb[ba][:, 0:N], in1=gT[:, oc, 0:N],
                                                             op=ALU.mult), r=["gT"], w=[PB(ba), "tA"])
                P.dve(lambda e, oc=oc, bb=bb: e.tensor_tensor(out=tB[:, 0:N], in0=pb[bb][:, 0:N],
                                                             in1=gT[:, 8 + oc, 0:N], op=ALU.mult),
                      r=["gT"], w=[PB(bb), "tB"])
                P.pool(lambda e, oc=oc: e.tensor_tensor(out=mT[:, oc, 0:N], in0=tA[:, 0:N], in1=tB[:, 0:N], op=ALU.add),
                       r=["tA", "tB"], w=["mT"])
            if STOP == "p2c":
                return dump()
            wov, wok = wload(wo_s[l], 128, (8, 1024))
            for s in range(nsub):
                for hc in range(2):
                    bk = nbank(0, 4)
                    for oc in range(8):
                        P.pe(lambda e, s=s, hc=hc, oc=oc, bk=bk: e.matmul(
                            pb[bk][0:n, :], lhsT=mT[:, oc, s * 128:s * 128 + n], rhs=wov[:, oc, hc * 512:(hc + 1) * 512],
                            start=(oc == 0), stop=(oc == 7)), r=[wok + (0,), wok + (1,), "mT"], w=[PB(bk)])
                    P.dve(lambda e, s=s, hc=hc, bk=bk: e.tensor_tensor(
                        out=xt[0:n, s, hc * 512:(hc + 1) * 512], in0=xt[0:n, s, hc * 512:(hc + 1) * 512],
                        in1=pb[bk][0:n, :], op=ALU.add), w=[PB(bk), "xt"])
            if STOP == "p2d":
                return dump()
            for s in range(nsub):
                norm_transpose(xt[0:n, s, :], n, g2t, h2T[:, :, s * 128:s * 128 + n], "h2T", "xt")
            if STOP == "p2e":
                return dump()
            for q in range(4):
                wv, wk = wload(w1_s[l][:, :, q * 1024:(q + 1) * 1024], 128, (8, 1024))
                for fcl in range(8):
                    fc = q * 8 + fcl
                    bk = nbank(0, 4)
                    for j in range(8):
                        P.pe(lambda e, wv=wv, fcl=fcl, j=j, bk=bk: e.matmul(
                            pb[bk][:, 0:N], lhsT=wv[:, j, fcl * 128:(fcl + 1) * 128], rhs=h2T[:, j, 0:N],
                            start=(j == 0), stop=(j == 7)), r=[wk + (0,), wk + (1,), "h2T"], w=[PB(bk)])
                    ri = fc % 2
                    P.act(lambda e, bk=bk, ri=ri: e.activation(out=rbuf[ri][:, 0:N], in_=pb[bk][:, 0:N], func=AF.Relu),
                          w=[PB(bk), ("rbuf", ri)])
                    P.pool(lambda e, fc=fc, ri=ri: e.tensor_tensor(out=u2T[:, fc, 0:N], in0=rbuf[ri][:, 0:N],
                                                                  in1=rbuf[ri][:, 0:N], op=ALU.mult),
                           r=[("rbuf", ri)], w=["u2T"])
            if STOP == "p2f":
                return dump()
            for hc in range(2):
                wv, wk = wload(w2_s[l][:, :, hc * 512:(hc + 1) * 512], 128, (32, 512))
                for s in range(nsub):
                    bk = nbank(4, 8)
                    for fc in range(32):
                        P.pe(lambda e, wv=wv, s=s, fc=fc, bk=bk: e.matmul(
                            pb[bk][0:n, :], lhsT=u2T[:, fc, s * 128:s * 128 + n], rhs=wv[:, fc, :],
                            start=(fc == 0), stop=(fc == 31)), r=[wk + (0,), wk + (1,), "u2T"], w=[PB(bk)])
                    P.dve(lambda e, s=s, hc=hc, bk=bk: e.tensor_tensor(
                        out=xt[0:n, s, hc * 512:(hc + 1) * 512], in0=xt[0:n, s, hc * 512:(hc + 1) * 512],
                        in1=pb[bk][0:n, :], op=ALU.add), w=[PB(bk), "xt"])
            if STOP == "p2g":
                return dump()
            if last_layer:
                if c0 < TPC:
                    P.dma("pool", y_p[c0:c0 + N, :].rearrange("(s p) d -> p s d", p=128), xt[:, 0:nsub, :], r=["xt"])
                else:
                    P.dma("pool", y_s[:, :], xt[0:n, 0, :], r=["xt"])
            else:
                if N >= 128:
                    P.dma("pool", x1[c0:c0 + N, :].rearrange("(s p) d -> p s d", p=128), xt[:, 0:nsub, :],
                          r=["xt"], w=["x1"])
                else:
                    P.dma("pool", x1[c0:c0 + N, :], xt[0:n, 0, :], r=["xt"], w=["x1"])
                emit_hT(l + 1, c0, N, nsub, n)
            if STOP == "p2h":
                return dump()

    for l in range(L):
        P.barrier(bscr[:])
        p1(l)
        if DBG and l == 0:
            P.dma("sp", dbg_o, ag_o_in_a, r=["ag_o_in"])
        if STOP == "p1":
            break
        allgather(ag_o_in, ag_o_out, "ag_o_in", "ag_o_out")
        if STOP == "ag2":
            break
        P.barrier(bscr[:])
        p2(l)
        if l + 1 < L:
            allgather(ag_h_in, ag_h_out, "ag_h_in", "ag_h_out")
    P.emit()
    return nc


def _consts(cfg, c):
    S, PAST = cfg["SEQ"], cfg["PAST"]
    NKT = S // 128
    bf = ml_dtypes.bfloat16
    slope = 2.0 ** (-(c + 1))
    kk = np.arange(S) % 128
    kaugS = np.stack([8.0 * slope * kk, np.ones(S), np.ones(S)]).astype(np.float32).astype(bf)
    qr = np.arange(512)
    qaug = np.stack([np.ones(512), -8.0 * slope * (qr & 255), -8.0 * slope * (qr & 256)]).astype(np.float32).astype(bf)
    qs = np.arange(256) % 16
    qaugs = np.stack([np.ones(256), -8.0 * slope * qs, np.zeros(256)]).astype(np.float32).astype(bf)
    m = np.arange(NKT + 3) - 3
    biasA = np.broadcast_to((-slope * 128.0 * m)[None, :], (128, NKT + 3)).astype(np.float32).copy()
    kr = np.arange(128)[:, None]
    qq = np.arange(128)[None, :]
    cb = np.zeros((128, 7, 128), np.float32)
    for d in range(-3, 4):
        if d < 0:
            cb[:, d + 3, :] = -240000.0
        elif d == 0:
            same = (kr // 64) == (qq // 64)
            later = (kr // 64) > (qq // 64)
            blk = np.where(later, -240000.0, np.where(same & (kr > qq), -16.0 * slope * (kr - qq), 0.0))
            cb[:, 3, :] = blk
    cblk = cb.reshape(128, 7 * 128).astype(bf)
    eye = np.eye(128, dtype=np.float32)
    one = np.ones((128, 128), np.float32)
    return dict(kaugS=kaugS, qaug=qaug, qaugs=qaugs, biasA=biasA, cblk=cblk, identb=eye.astype(bf), identf=eye,
                onesb=one.astype(bf), onesf=one)


def _bblocks(rel, PAST):
    NEG = np.float32(-30000.0)
    kr = np.arange(128)[:, None]
    qq = np.arange(128)[None, :]
    out = np.empty((128, 11, 128), np.float32)
    for d in range(-7, 4):
        delta = 128 * d + 512 + qq - kr
        dch = (128 * d + 512 + qq) // 64 - kr // 64
        vis = (dch >= 0) & (dch <= 8)
        idx = np.clip(delta, -128, 128) + 128
        out[:, d + 7, :] = np.where(vis, rel[idx], NEG)
    sm = np.full((128, 5, 16), NEG, np.float32)
    qi = np.arange(16)[None, :]
    for t in range(5):
        nk = 128 if t < 4 else 16
        kpos = (PAST - 512 + 128 * t + np.arange(nk))[:, None] if t < 4 else (PAST + np.arange(16))[:, None]
        delta = (PAST + qi) - kpos
        dch = (PAST + qi) // 64 - kpos // 64
        vis = (dch >= 0) & (dch <= 8)
        idx = np.clip(delta, -128, 128) + 128
        sm[0:nk, t, :] = np.where(vis, rel[idx], NEG)
    return out.reshape(128, 11 * 128), sm.reshape(128, 80)


def _run(cfg, inp):
    S, PAST, L = cfg["SEQ"], cfg["PAST"], cfg["L"]
    NTOK = 2 * S
    TPC = NTOK // NC
    f = lambda k: np.asarray(inp[k], dtype=np.float32)
    x_prompt = f("x_prompt").reshape(NTOK, D)
    x_sample = f("x_sample").reshape(256, D)
    w_in = f("w_in")
    rep = lambda v: np.ascontiguousarray(np.broadcast_to(v[:, None, :], (v.shape[0], 128, v.shape[1])))
    ones64 = np.ones((L, 64), np.float32)
    in_maps = []
    for c in range(NC):
        cols = np.concatenate([np.arange(128 * c, 128 * c + 128), 1024 + np.arange(128 * c, 128 * c + 128),
                               2048 + np.arange(128 * c, 128 * c + 128), 3072 + np.arange(64 * c, 64 * c + 64),
                               3584 + np.arange(64 * c, 64 * c + 64), 4096 + np.arange(64 * c, 64 * c + 64)])
        qkg = np.concatenate([f("qn_a_g"), f("qn_a_g"), f("kn_a_g"), f("kn_a_g"), ones64, ones64, f("qn_b_g"),
                              f("kn_b_g")], axis=1)
        lamv = np.concatenate([f("lam_q1"), f("lam_k1"), f("lam_q2"), f("lam_k2")], axis=1)
        bb = [_bblocks(f("rel_bias_b")[l, c], PAST) for l in range(L)]
        d = dict(
            xp=np.ascontiguousarray(x_prompt[c * TPC:(c + 1) * TPC]),
            xs=np.ascontiguousarray(x_sample[c * 32:(c + 1) * 32]),
            win=np.ascontiguousarray(w_in[:, :, cols]),
            g1b=rep(f("norm1_g")), g2b=rep(f("norm2_g")), qkg=rep(qkg), lamv=rep(lamv),
            sublng=np.ascontiguousarray(f("subln_a_g")[:, :, None]),
            bblk=np.stack([b[0] for b in bb]), bblks=np.stack([b[1] for b in bb]),
            cak=np.ascontiguousarray(f("cache_a_k")[:, :, :, c, :]), cav=np.ascontiguousarray(f("cache_a_v")[:, :, :, c, :]),
            cbk=np.ascontiguousarray(f("cache_b_k")[:, :, :, c, :]), cbv=np.ascontiguousarray(f("cache_b_v")[:, :, :, c, :]),
            wbra=f("w_br_a"), wbrb=f("w_br_b"), wgate=f("w_gate"), wout=f("w_out"), wff1=f("w_ff1"), wff2=f("w_ff2"),
        )
        d.update(_consts(cfg, c))
        in_maps.append(d)
    nc = build(cfg)
    res = run_bass_kernel_spmd(nc, in_maps, core_ids=list(range(NC)))
    R = res.results
    if cfg.get("DBG"):
        global DBG_OUT
        DBG_OUT = [{k: np.asarray(v) for k, v in R[c].items() if k.startswith("dbg_")} for c in range(NC)]
    g = lambda k: [np.asarray(R[c][k], dtype=np.float32) for c in range(NC)]
    y_p = np.concatenate(g("y_p"), 0).reshape(2, S, D)
    y_s = np.concatenate(g("y_s"), 0).reshape(16, 16, D)
    hs = lambda k, shp: np.stack(g(k), axis=-2).reshape(shp)
    ak_p = hs("ak_p", (L, 2, S, 8, 128)); av_p = hs("av_p", (L, 2, S, 8, 128))
    bk_p = hs("bk_p", (L, 2, 512, 8, 64)); bv_p = hs("bv_p", (L, 2, 512, 8, 64))
    ak_s = hs("ak_s", (L, 16, 16, 8, 128)); av_s = hs("av_s", (L, 16, 16, 8, 128))
    bk_s = hs("bk_s", (L, 16, 16, 8, 64)); bv_s = hs("bv_s", (L, 16, 16, 8, 64))
    return (y_p, y_s, ak_p, av_p, bk_p, bv_p, ak_s, av_s, bk_s, bv_s)


def kernel(**inputs):
    cfg = dict(SEQ=16384, PAST=2048, L=2)
    return _run(cfg, inputs)
```

```python
import contextlib
import math
import os

import numpy as np
import ml_dtypes

import concourse.bass as bass
import concourse.mybir as mybir
from concourse.bass_utils import run_bass_kernel_spmd

F32 = mybir.dt.float32
BF16 = mybir.dt.bfloat16
AF = mybir.ActivationFunctionType
ALU = mybir.AluOpType
AX = mybir.AxisListType
EPS = 1e-6
NC = 8
D = 1024
DFF = 4096


class _Op:
    __slots__ = ("eng", "fn", "deps", "kind", "sig", "signal", "slot", "slotcnt", "idx")


class _Rec:
    def __init__(self):
        self.calls = []

    def __getattr__(self, name):
        def f(*a, **k):
            self.calls.append((name, a, k))
            return self
        return f


class Prog:
    CAP = 1500
    RING = {"sp": 16, "pool": 16, "act": 4}
    COMPUTE = ("pe", "act", "dve", "pool")

    def __init__(self, nc):
        self.nc = nc
        self.ops = []
        self.last_w = {}
        self.readers = {}
        self.ndma = {"sp": 0, "pool": 0, "act": 0}

    def add(self, eng, fn, r=(), w=(), kind="c", late=False):
        if not late:
            rec = _Rec()
            fn(rec)
            assert len(rec.calls) == 1
            fn = (lambda nm, a, k: (lambda E: getattr(E, nm)(*a, **k)))(*rec.calls[0])
        op = _Op()
        op.idx = len(self.ops)
        op.eng, op.fn, op.kind = eng, fn, kind
        op.signal = kind != "c"
        op.sig = None
        r = tuple(r) + ("ARENA",)
        deps = set()
        for k in r + tuple(w):
            lw = self.last_w.get(k)
            if lw is not None:
                deps.add(lw)
        for k in w:
            rd = self.readers.get(k)
            if rd:
                deps.update(rd.values())
        for k in w:
            self.last_w[k] = op.idx
            self.readers[k] = {}
        for k in r:
            d = self.readers.setdefault(k, {})
            if kind == "c":
                d[("c", eng)] = op.idx
            else:
                d[("d", op.idx)] = op.idx
        deps.discard(op.idx)
        best = {}
        out = []
        for i in deps:
            o = self.ops[i]
            if o.kind == "c":
                if o.eng == "pe" and eng == "pe" and kind == "c":
                    continue
                if best.get(o.eng, -1) < i:
                    best[o.eng] = i
            else:
                out.append(i)
        out.extend(best.values())
        out.sort(reverse=True)
        op.deps = out
        for i in out:
            self.ops[i].signal = True
        if kind == "d":
            n = self.ndma[eng]
            self.ndma[eng] = n + 1
            R = self.RING[eng]
            op.slot = n % R
            op.slotcnt = n // R + 1
        self.ops.append(op)
        return op.idx

    def pe(self, fn, r=(), w=()):
        return self.add("pe", fn, r, w)

    def act(self, fn, r=(), w=()):
        return self.add("act", fn, r, w)

    def dve(self, fn, r=(), w=()):
        return self.add("dve", fn, r, w)

    def pool(self, fn, r=(), w=()):
        return self.add("pool", fn, r, w)

    def dma(self, q, out, in_, r=(), w=()):
        return self.add(q, lambda e: e.dma_start(out=out, in_=in_), r, w, kind="d")

    def barrier(self, scratch):
        self.add("pool", lambda e: e.memset(scratch, 0.0), r=(), w=("ARENA", "barrier_scratch"))

    def emit(self):
        nc = self.nc
        ops = self.ops
        cnt = {e: 0 for e in self.COMPUTE}
        for op in ops:
            if op.kind == "c" and op.signal:
                c = cnt[op.eng]
                cnt[op.eng] = c + 1
                op.sig = (op.eng, c // self.CAP, c % self.CAP + 1)
        sems = {}
        with contextlib.ExitStack() as st:
            for e in self.COMPUTE:
                for ep in range(cnt[e] // self.CAP + 1):
                    sems[(e, ep)] = st.enter_context(nc.semaphore(f"s_{e}_{ep}"))
            for q, R in self.RING.items():
                for s in range(min(R, self.ndma[q])):
                    sems[("dma", q, s)] = st.enter_context(nc.semaphore(f"d_{q}_{s}"))
            k = 0
            for op in ops:
                if op.kind == "cc":
                    sems[("cc", op.idx)] = st.enter_context(nc.semaphore(f"cc_{k}"))
                    k += 1
            block = st.enter_context(nc.Block())
            per_eng = {e: [] for e in ("pe", "act", "dve", "pool", "sp")}
            for op in ops:
                per_eng[op.eng].append(op)

            def target(op):
                if op.kind == "c":
                    e, ep, v = op.sig
                    return ("c", e), sems[(e, ep)], (ep, v)
                if op.kind == "d":
                    return ("d", op.eng, op.slot), sems[("dma", op.eng, op.slot)], (0, 16 * op.slotcnt)
                return ("cc", op.idx), sems[("cc", op.idx)], (0, 1)

            def run(engname, E):
                seen = {}
                if engname == "sp":
                    self.pid = E.partition_id()
                for op in per_eng[engname]:
                    for i in op.deps:
                        key, sem, val = target(ops[i])
                        if seen.get(key, (-1, -1)) >= val:
                            continue
                        E.wait_ge(sem, val[1])
                        seen[key] = val
                    if op.kind == "d" and op.slotcnt > 1:
                        key = ("d", op.eng, op.slot)
                        val = (0, 16 * (op.slotcnt - 1))
                        if seen.get(key, (-1, -1)) < val:
                            E.wait_ge(sems[("dma", op.eng, op.slot)], val[1])
                            seen[key] = val
                    ins = op.fn(E)
                    if op.kind == "c":
                        if op.signal:
                            ins.then_inc(sems[(op.sig[0], op.sig[1])], 1)
                    elif op.kind == "d":
                        ins.then_inc(sems[("dma", op.eng, op.slot)], 16)
                    else:
                        ins.then_inc(sems[("cc", op.idx)], 1)
                if engname == "sp":
                    for q, R in self.RING.items():
                        n = self.ndma[q]
                        for s in range(min(R, n)):
                            c = (n - 1 - s) // R + 1
                            E.wait_ge(sems[("dma", q, s)], 16 * c)
                    for op in ops:
                        if op.kind == "cc":
                            E.wait_ge(sems[("cc", op.idx)], 1)

            @block.tensor
            def _(E):
                run("pe", E)

            @block.scalar
            def _(E):
                run("act", E)

            @block.vector
            def _(E):
                run("dve", E)

            @block.gpsimd
            def _(E):
                run("pool", E)

            @block.sync
            def _(E):
                run("sp", E)


class SBAlloc:
    def __init__(self, nc, base=16512, limit=229376):
        self.nc, self.off, self.limit = nc, base, limit

    def __call__(self, name, shape, dt):
        n = 1
        for s in shape[1:]:
            n *= s
        size = n * (4 if dt == F32 else 2)
        size = (size + 63) // 64 * 64
        t = self.nc.alloc_sbuf_tensor_at(name, list(shape), dt, offset=self.off)
        self.off += size
        assert self.off <= self.limit, (name, self.off)
        return t


def build(cfg):
    S, PAST, L = cfg["SEQ"], cfg["PAST"], cfg["L"]
    NTOK = 2 * S
    TPC = NTOK // NC
    NST = NTOK // 512
    STB = S // 512
    NKT = S // 128
    MYT = TPC + 32
    NALL = NTOK + 256
    NCT = PAST // 128
    TPR = TPC // 512
    assert TPC % 512 == 0 and PAST % 1024 == 0 or PAST == 512

    nc = bass.Bass("TRN2", target_bir_lowering=False)

    def din(name, shape, dt=F32):
        return nc.dram_tensor(name, list(shape), dt, kind="ExternalInput").ap()

    def dout(name, shape, dt=F32):
        return nc.dram_tensor(name, list(shape), dt, kind="ExternalOutput").ap()

    def dint(name, shape, dt):
        return nc.dram_tensor(name, list(shape), dt)

    xp = din("xp", [TPC, D]); xs = din("xs", [32, D])
    win = din("win", [L, D, 576])
    g1b = din("g1b", [L, 128, D]); g2b = din("g2b", [L, 128, D])
    qkg = din("qkg", [L, 128, 512]); lamv = din("lamv", [L, 128, 256]); sublng = din("sublng", [L, 128, 1])
    bblk = din("bblk", [L, 128, 11 * 128]); bblks = din("bblks", [L, 128, 80])
    cak = din("cak", [L, 16, PAST, 128]); cav = din("cav", [L, 16, PAST, 128])
    cbk = din("cbk", [L, 16, 512, 64]); cbv = din("cbv", [L, 16, 512, 64])
    wbra = din("wbra", [L, D, D]); wbrb = din("wbrb", [L, 512, D]); wgate = din("wgate", [L, D, 2 * D])
    wout = din("wout", [L, D, D]); wff1 = din("wff1", [L, D, DFF]); wff2 = din("wff2", [L, DFF, D])
    kaugS = din("kaugS", [3, S], BF16); qaug = din("qaug", [3, 512], BF16); qaugs = din("qaugs", [3, 256], BF16)
    biasA_d = din("biasA", [128, NKT + 3]); cblk_d = din("cblk", [128, 7 * 128], BF16)
    identb_d = din("identb", [128, 128], BF16); identf_d = din("identf", [128, 128])
    onesb_d = din("onesb", [128, 128], BF16); onesf_d = din("onesf", [128, 128])

    y_p = dout("y_p", [TPC, D]); y_s = dout("y_s", [32, D])
    ak_p = dout("ak_p", [L, NTOK, 128]); av_p = dout("av_p", [L, NTOK, 128])
    bk_p = dout("bk_p", [L, 2, 512, 64]); bv_p = dout("bv_p", [L, 2, 512, 64])
    ak_s = dout("ak_s", [L, 256, 128]); av_s = dout("av_s", [L, 256, 128])
    bk_s = dout("bk_s", [L, 256, 64]); bv_s = dout("bv_s", [L, 256, 64])

    DBG = cfg.get("DBG", False)
    if DBG:
        dbg_o = dout("dbg_o", [192, NALL], BF16)
        dbg_xt = dout("dbg_xt", [128, 4096], F32); dbg_mT = dout("dbg_mT", [128, 4096], BF16)
        dbg_gT = dout("dbg_gT", [128, 8192], BF16); dbg_oaT = dout("dbg_oaT", [128, 4096], BF16)
        dbg_obT = dout("dbg_obT", [64, 4096], BF16)
    ag_h_in = dint("ag_h_in", [D, MYT], BF16); ag_h_out = dint("ag_h_out", [NC * D, MYT], BF16)
    ag_o_in = dint("ag_o_in", [192, NALL], BF16); ag_o_out = dint("ag_o_out", [NC * 192, NALL], BF16)
    x1 = dint("x1", [MYT, D], F32).ap()
    wg_s = dint("wg_s", [L, 128, 8, 2048], BF16).ap(); wa_s = dint("wa_s", [L, 128, 8, 1024], BF16).ap()
    wb_s = dint("wb_s", [L, 64, 8, 1024], BF16).ap(); wo_s = dint("wo_s", [L, 128, 8, 1024], BF16).ap()
    w1_s = dint("w1_s", [L, 128, 8, 4096], BF16).ap(); w2_s = dint("w2_s", [L, 128, 32, 1024], BF16).ap()
    ag_h_in_v = ag_h_in.ap().rearrange("(j p) t -> p j t", p=128)
    ag_h_out_v = ag_h_out.ap().rearrange("(r j p) t -> p r j t", r=NC, j=8, p=128)
    ag_o_in_a = ag_o_in.ap()
    ag_o_out_v = ag_o_out.ap().rearrange("(h e) t -> e h t", e=192)

    P = Prog(nc)
    pb = [nc.alloc_psum_tensor(f"pb{i}", [128, 512], F32) for i in range(8)]
    PB = lambda i: ("pb", i)

    A = SBAlloc(nc)
    identb = A("identb", [128, 128], BF16); identf = A("identf", [128, 128], F32)
    onesb = A("onesb", [128, 128], BF16); onesf = A("onesf", [128, 128], F32)
    biasA = A("biasA", [128, NKT + 3], F32); cblk = A("cblk", [128, 7 * 128], BF16)
    g1t = A("g1t", [128, D], F32); g2t = A("g2t", [128, D], F32)
    qkgt = A("qkgt", [128, 512], F32); lamt = A("lamt", [128, 256], F32)
    small = A("small", [128, 32], F32)
    bscr = A("bscr", [128, 8], F32)
    ARENA0 = A.off

    for t, d in ((identb, identb_d), (identf, identf_d), (onesb, onesb_d), (onesf, onesf_d),
                 (biasA, biasA_d), (cblk, cblk_d)):
        P.dma("sp", t[:], d, w=[t.name])
    CONST_R = [identb.name, identf.name, onesb.name, onesf.name, biasA.name, cblk.name]

    cast_rr = [0]

    def cast_copy(out, in_, r, w):
        k = cast_rr[0] % 3
        cast_rr[0] += 1
        if k == 0:
            P.dve(lambda e: e.tensor_copy(out=out, in_=in_), r=r, w=w)
        elif k == 1:
            P.act(lambda e: e.copy(out=out, in_=in_), r=r, w=w)
        else:
            P.pool(lambda e: e.tensor_copy(out=out, in_=in_), r=r, w=w)

    A = SBAlloc(nc, ARENA0)
    stg32 = [A(f"stg32_{i}", [128, 2048], F32) for i in range(2)]
    stg16 = [A(f"stg16_{i}", [128, 2048], BF16) for i in range(2)]
    pcount = [0]

    def prep(src_v, dst_v, np_, Adim, C):
        if C >= 2048:
            astep, cstep = 1, 2048
        else:
            astep, cstep = 2048 // C, C
        for a0 in range(0, Adim, astep):
            for c0 in range(0, C, cstep):
                i = pcount[0] % 2
                pcount[0] += 1
                s32 = stg32[i][0:np_, :].rearrange("p (a c) -> p a c", a=astep)
                s16 = stg16[i][0:np_, :].rearrange("p (a c) -> p a c", a=astep)
                P.dma("sp", s32, src_v[:, a0:a0 + astep, c0:c0 + cstep], w=[("stg32", i)])
                cast_copy(s16, s32, r=[("stg32", i)], w=[("stg16", i)])
                P.dma("pool", dst_v[:, a0:a0 + astep, c0:c0 + cstep], s16, r=[("stg16", i)])

    for l in range(L):
        prep(wgate[l].rearrange("(j p) c -> p j c", p=128), wg_s[l], 128, 8, 2048)
        prep(wbra[l].rearrange("(h e) c -> e h c", e=128), wa_s[l], 128, 8, 1024)
        prep(wbrb[l].rearrange("(h e) c -> e h c", e=64), wb_s[l], 64, 8, 1024)
        prep(wout[l].rearrange("(j p) c -> p j c", p=128), wo_s[l], 128, 8, 1024)
        prep(wff1[l].rearrange("(j p) c -> p j c", p=128), w1_s[l], 128, 8, 4096)
        prep(wff2[l].rearrange("(j p) c -> p j c", p=128), w2_s[l], 128, 32, 1024)
    P.barrier(bscr[:])
    STOP = cfg.get("STOP", "")
    if STOP == "prep":
        P.emit()
        return nc

    A = SBAlloc(nc, ARENA0)
    xt = A("xt", [128, 4, D], F32)
    hT = A("hT", [128, 8, 512], BF16)
    oaT = A("oaT", [128, 8, 512], BF16)
    obT = A("obT", [64, 8, 512], BF16)
    gT = A("gT", [128, 16, 512], BF16)
    mT = A("mT", [128, 8, 512], BF16)
    h2T = A("h2T", [128, 8, 512], BF16)
    u2T = A("u2T", [128, 32, 512], BF16)
    wring = [A(f"wring{i}", [128, 8192], BF16) for i in range(4)]
    hb32 = A("hb32", [128, D], F32)
    junk = A("junk", [128, D], BF16)
    tA = A("tA2", [128, 512], F32); tB = A("tB2", [128, 512], F32)
    rbuf = [A(f"rbuf{i}", [128, 512], BF16) for i in range(2)]
    P2_END = A.off

    bankrr = [0]

    def nbank(lo=0, hi=8):
        b = lo + bankrr[0] % (hi - lo)
        bankrr[0] += 1
        return b

    def norm_transpose(xsrc, n, gt, dst, dkey, xkey):
        ss = small[0:n, 2:3]
        rs = small[0:n, 3:4]
        P.act(lambda e: e.activation(out=junk[0:n, :], in_=xsrc, func=AF.Square, scale=1.0 / math.sqrt(D),
                                     accum_out=ss), r=[xkey], w=["junk", "nt_ss"])
        P.dve(lambda e: e.tensor_scalar(out=rs, in0=ss, scalar1=1.0, scalar2=EPS, op0=ALU.mult, op1=ALU.add),
              r=["nt_ss"], w=["nt_rs"])
        P.act(lambda e: e.activation(out=rs, in_=rs, func=AF.Sqrt), w=["nt_rs"])
        P.dve(lambda e: e.reciprocal(out=rs, in_=rs), w=["nt_rs"])
        P.dve(lambda e: e.scalar_tensor_tensor(out=hb32[0:n, :], in0=xsrc, scalar=rs, in1=gt[0:n, :],
                                               op0=ALU.mult, op1=ALU.mult),
              r=[xkey, "nt_rs", gt.name], w=["hb32"])
        for half in range(2):
            bk = 4 + half
            for jj in range(4):
                j = half * 4 + jj
                P.pe(lambda e, j=j, jj=jj, bk=bk: e.transpose(out=pb[bk][:, jj * 128:jj * 128 + n],
                                                             in_=hb32[0:n, j * 128:(j + 1) * 128],
                                                             identity=identf[0:n, 0:n]),
                     r=["hb32", identf.name], w=[PB(bk)])
            src = pb[bk][:, :].rearrange("p (j t) -> p j t", j=4)[:, :, 0:n]
            d = dst[:, half * 4:half * 4 + 4, :]
            if half == 0:
                P.act(lambda e, d=d, src=src: e.copy(out=d, in_=src), w=[PB(bk), dkey])
            else:
                P.dve(lambda e, d=d, src=src: e.tensor_copy(out=d, in_=src), w=[PB(bk), dkey])

    def my_tiles():
        tl = [(i * 512, 512) for i in range(TPC // 512)]
        tl.append((TPC, 32))
        return tl

    def load_x(l, c0, N):
        src = (xp if c0 < TPC else xs) if l == 0 else None
        nsub = max(1, N // 128)
        n = min(N, 128)
        if l == 0:
            if c0 < TPC:
                P.dma("sp", xt[:, 0:nsub, :], xp[c0:c0 + N, :].rearrange("(s p) d -> p s d", p=128), w=["xt"])
            else:
                P.dma("sp", xt[0:n, 0, :], xs[:, :], w=["xt"])
        else:
            if N >= 128:
                P.dma("sp", xt[:, 0:nsub, :], x1[c0:c0 + N, :].rearrange("(s p) d -> p s d", p=128),
                      r=["x1"], w=["xt"])
            else:
                P.dma("sp", xt[0:n, 0, :], x1[c0:c0 + N, :], r=["x1"], w=["xt"])
        return nsub, n

    def emit_hT(l_next, c0, N, nsub, n):
        for s in range(nsub):
            norm_transpose(xt[0:n, s, :], n, g1t, hT[:, :, s * 128:s * 128 + n], "hT", "xt")
        P.dma("pool", ag_h_in_v[:, :, c0:c0 + N], hT[:, :, 0:N], r=["hT"], w=["ag_h_in"])

    def allgather(src, dst, rk, wk):
        P.add("pool", lambda e: e.collective_compute("AllGather", ALU.bypass, replica_groups=[list(range(NC))],
                                                     ins=[src.ap().opt()], outs=[dst.ap().opt()]),
              r=[rk], w=[wk], kind="cc")

    P.dma("sp", g1t[:], g1b[0], w=[g1t.name])
    for (c0, N) in my_tiles():
        nsub, n = load_x(0, c0, N)
        emit_hT(0, c0, N, nsub, n)
    if STOP == "p0":
        P.emit()
        return nc
    allgather(ag_h_in, ag_h_out, "ag_h_in", "ag_h_out")
    if STOP == "ag1":
        P.emit()
        return nc

    def p1_layout():
        A = SBAlloc(nc, ARENA0)
        o = {}
        o["KaT"] = [A(f"KaT{i}", [67, S], BF16) for i in range(2)]
        o["Va"] = A("Va", [128, NKT, 128], BF16)
        o["hTs"] = [A(f"hTs{i}", [128, 8, 512], BF16) for i in range(2)]
        o["wc"] = A("wc", [128, 8, 576], BF16)
        o["bblkt"] = A("bblkt", [128, 11 * 128], F32)
        o["bblkst"] = A("bblkst", [128, 80], F32)
        o["QaT"] = [A(f"QaT{i}", [67, 512], BF16) for i in range(2)]
        o["QbT"] = A("QbT", [64, 512], BF16)
        o["KbT"] = A("KbT", [64, 1024], BF16)
        o["Vb"] = A("Vb", [128, 8, 64], BF16)
        o["zs"] = [A(f"zs{i}", [128, 576], F32) for i in range(2)]
        o["sq"] = [A(f"sq{i}", [128, 512], F32) for i in range(2)]
        o["kn32"] = [A(f"kn32{i}", [128, 512], F32) for i in range(2)]
        o["Pa"] = [[A(f"Pa{m}_{i}", [128, 512], BF16) for i in range(2)] for m in range(2)]
        o["Pb"] = [A(f"Pb{i}", [128, 512], BF16) for i in range(2)]
        o["sb32"] = A("sb32", [128, 512], F32)
        o["fA"] = A("fA", [128, 512], F32); o["fB"] = A("fB", [128, 512], F32)
        o["obf"] = A("obf", [128, 512], BF16)
        o["KsT"] = [A(f"KsT{i}", [67, (NCT + 1) * 128], BF16) for i in range(2)]
        o["Vs"] = A("Vs", [128, NCT + 1, 128], BF16)
        o["ck"] = A("ck", [128, 8, 128], F32)
        o["cv"] = A("cv", [128, 8, 128], F32)
        o["wst"] = A("wst", [128, 576], F32)
        o["QsT"] = [A(f"QsT{i}", [67, 256], BF16) for i in range(2)]
        o["KnT"] = [A(f"KnT{i}", [64, 256], BF16) for i in range(2)]
        o["QbsT"] = A("QbsT", [64, 256], BF16)
        o["KbnT"] = A("KbnT", [64, 256], BF16)
        o["KbsT"] = A("KbsT", [64, 640], BF16)
        o["Vbs"] = A("Vbs", [128, 5, 64], BF16)
        o["cbkt"] = A("cbkt", [128, 4, 64], F32); o["cbvt"] = A("cbvt", [128, 4, 64], F32)
        o["vas"] = [A(f"vas{i}", [128, 128], BF16) for i in range(2)]
        o["vbs"] = [A(f"vbs{i}", [128, 64], BF16) for i in range(2)]
        o["Vsn"] = A("Vsn", [16, 128], BF16); o["Vbsn"] = A("Vbsn", [16, 64], BF16)
        return o

    T1 = p1_layout()

    def p1(l):
        lam_init = 0.8 - 0.6 * math.exp(-0.3 * l)
        KaT, Va, hTs, wc = T1["KaT"], T1["Va"], T1["hTs"], T1["wc"]
        QaT, QbT, KbT, Vb = T1["QaT"], T1["QbT"], T1["KbT"], T1["Vb"]
        zsL, sqL, kn32L = T1["zs"], T1["sq"], T1["kn32"]
        sq = sqL[0]
        Pa, Pbb, sb32, fA, fB, obf = T1["Pa"], T1["Pb"], T1["sb32"], T1["fA"], T1["fB"], T1["obf"]
        bblkt, bblkst = T1["bblkt"], T1["bblkst"]
        for j in range(8):
            P.dma("sp", T1["wst"][:], win[l, j * 128:(j + 1) * 128, :], w=["wst"])
            cast_copy(wc[:, j, :], T1["wst"][:], r=["wst"], w=["wc"])
        P.dma("sp", qkgt[:], qkg[l], w=["qkgt"])
        P.dma("sp", lamt[:], lamv[l], w=["lamt"])
        P.dma("sp", small[:, 1:2], sublng[l], w=["gcol"])
        P.dma("sp", bblkt[:], bblk[l], w=["bblkt"])
        P.dma("sp", bblkst[:], bblks[l], w=["bblkst"])
        for m in range(2):
            P.dma("sp", KaT[m][64:67, :], kaugS, w=[("KaT", m)])
            P.dma("sp", QaT[m][64:67, :], qaug, w=[("QaT", m)])
            P.dma("sp", T1["KsT"][m][64:67, :], kaugS[:, 0:(NCT + 1) * 128], w=[("KsT", m)])
            P.dma("sp", T1["QsT"][m][64:67, :], qaugs, w=[("QsT", m)])
        P.dve(lambda e: e.tensor_tensor(out=sq[:, 0:64], in0=lamt[:, 0:64], in1=lamt[:, 64:128], op=ALU.mult),
              r=["lamt"], w=[("sq", 0)])
        P.dve(lambda e: e.tensor_tensor(out=sq[:, 64:128], in0=lamt[:, 128:192], in1=lamt[:, 192:256], op=ALU.mult),
              r=["lamt"], w=[("sq", 0)])
        P.dve(lambda e: e.tensor_reduce(out=small[:, 4:6], in_=sq[:, 0:128].rearrange("p (a b) -> p a b", a=2),
                                        axis=AX.X, op=ALU.add), r=[("sq", 0)], w=["lam_s"])
        P.act(lambda e: e.activation(out=small[:, 6:8], in_=small[:, 4:6], func=AF.Exp), r=["lam_s"], w=["lam_e"])
        P.dve(lambda e: e.tensor_tensor(out=small[:, 0:1], in0=small[:, 7:8], in1=small[:, 6:7], op=ALU.subtract),
              r=["lam_e"], w=["neglam"])
        P.dve(lambda e: e.tensor_scalar(out=small[:, 0:1], in0=small[:, 0:1], scalar1=1.0, scalar2=-lam_init,
                                        op0=ALU.mult, op1=ALU.add), w=["neglam"])
        P.dve(lambda e: e.tensor_scalar(out=small[:, 1:2], in0=small[:, 1:2], scalar1=1.0 - lam_init, scalar2=0.0,
                                        op0=ALU.mult, op1=ALU.add), w=["gcol"])
        neglam = small[:, 0:1]
        gcol = small[:, 1:2]

        ZB = ((6, 7), (4, 5))

        def inproj_mm(hsl, tcol, p):
            b0, b1 = ZB[p]
            for j in range(8):
                P.pe(lambda e, j=j: e.matmul(pb[b0][:, :], lhsT=hTs[hsl][:, j, tcol:tcol + 128], rhs=wc[:, j, 0:512],
                                             start=(j == 0), stop=(j == 7)), r=[("hTs", hsl), "wc"], w=[PB(b0)])
            for j in range(8):
                P.pe(lambda e, j=j: e.matmul(pb[b1][:, 0:64], lhsT=hTs[hsl][:, j, tcol:tcol + 128],
                                             rhs=wc[:, j, 512:576], start=(j == 0), stop=(j == 7)),
                     r=[("hTs", hsl), "wc"], w=[PB(b1)])

        def inproj_chain(p):
            b0, b1 = ZB[p]
            zs, sq, kn32 = zsL[p], sqL[p], kn32L[p]
            kz, ksq, kkn, kss = ("zs", p), ("sq", p), ("kn32", p), ("ss8", p)
            P.act(lambda e: e.copy(out=zs[:, 0:512], in_=pb[b0][:, :]), w=[PB(b0), kz])
            P.act(lambda e: e.copy(out=zs[:, 512:576], in_=pb[b1][:, 0:64]), w=[PB(b1), kz])
            P.dve(lambda e: e.tensor_tensor(out=sq[:, :], in0=zs[:, 0:512], in1=zs[:, 0:512], op=ALU.mult),
                  r=[kz], w=[ksq])
            ss8 = small[:, 8 + 8 * p:16 + 8 * p]
            P.dve(lambda e: e.tensor_reduce(out=ss8, in_=sq[:, :].rearrange("p (a b) -> p a b", a=8), axis=AX.X,
                                            op=ALU.add), r=[ksq], w=[kss])
            P.dve(lambda e: e.tensor_scalar(out=ss8, in0=ss8, scalar1=1.0 / 64, scalar2=EPS, op0=ALU.mult,
                                            op1=ALU.add), w=[kss])
            P.act(lambda e: e.activation(out=ss8, in_=ss8, func=AF.Sqrt), w=[kss])
            P.dve(lambda e: e.reciprocal(out=ss8, in_=ss8), w=[kss])
            P.dve(lambda e: e.tensor_tensor(out=kn32[:, :].rearrange("p (a b) -> p a b", a=8),
                                            in0=zs[:, 0:512].rearrange("p (a b) -> p a b", a=8),
                                            in1=ss8.rearrange("p (a o) -> p a o", o=1).to_broadcast([128, 8, 64]),
                                            op=ALU.mult), r=[kz, kss], w=[kkn])
            P.pool(lambda e: e.tensor_tensor(out=kn32[:, :], in0=kn32[:, :], in1=qkgt[:, :], op=ALU.mult),
                   r=["qkgt"], w=[kkn])
            return zs, kn32, kz, kkn

        def transposes(dsts, kn32, kkn):
            grp = [0, 1, 2, 3, 6, 7]
            for idx in range(6):
                g = grp[idx]
                bk = 0 if idx < 4 else 1
                cc = (idx % 4) * 128
                P.pe(lambda e, g=g, bk=bk, cc=cc: e.transpose(out=pb[bk][0:64, cc:cc + 128],
                                                             in_=kn32[:, g * 64:(g + 1) * 64], identity=identf[:, :]),
                     r=[kkn, identf.name], w=[PB(bk)])
            for idx in range(6):
                bk = 0 if idx < 4 else 1
                cc = (idx % 4) * 128
                dst, key = dsts[idx]
                if idx % 2 == 0:
                    P.act(lambda e, dst=dst, bk=bk, cc=cc: e.copy(out=dst, in_=pb[bk][0:64, cc:cc + 128]),
                          w=[PB(bk), key])
                else:
                    P.dve(lambda e, dst=dst, bk=bk, cc=cc: e.tensor_copy(out=dst, in_=pb[bk][0:64, cc:cc + 128]),
                          w=[PB(bk), key])

        def finalize_A(N, col0):
            P.dve(lambda e: e.reciprocal(out=fA[:, 0:N], in_=pb[4][:, 0:N]), w=[PB(4), "fA"])
            P.dve(lambda e: e.tensor_tensor(out=fA[:, 0:N], in0=pb[2][:, 0:N], in1=fA[:, 0:N], op=ALU.mult),
                  w=[PB(2), "fA"])
            P.dve(lambda e: e.reciprocal(out=fB[:, 0:N], in_=pb[5][:, 0:N]), w=[PB(5), "fB"])
            P.dve(lambda e: e.tensor_tensor(out=fB[:, 0:N], in0=pb[3][:, 0:N], in1=fB[:, 0:N], op=ALU.mult),
                  w=[PB(3), "fB"])
            P.dve(lambda e: e.scalar_tensor_tensor(out=fA[:, 0:N], in0=fB[:, 0:N], scalar=neglam, in1=fA[:, 0:N],
                                                   op0=ALU.mult, op1=ALU.add), r=["neglam"], w=["fA", "fB"])
            P.act(lambda e: e.activation(out=fB[:, 0:N], in_=fA[:, 0:N], func=AF.Square), r=["fA"], w=["fB"])
            P.pe(lambda e: e.matmul(pb[0][:, 0:N], lhsT=onesf[:, :], rhs=fB[:, 0:N], start=True, stop=True),
                 r=["fB", onesf.name], w=[PB(0)])
            P.dve(lambda e: e.tensor_scalar(out=fB[:, 0:N], in0=pb[0][:, 0:N], scalar1=1.0 / 128, scalar2=EPS,
                                            op0=ALU.mult, op1=ALU.add), w=[PB(0), "fB"])
            P.act(lambda e: e.activation(out=fB[:, 0:N], in_=fB[:, 0:N], func=AF.Sqrt), w=["fB"])
            P.dve(lambda e: e.reciprocal(out=fB[:, 0:N], in_=fB[:, 0:N]), w=["fB"])
            P.dve(lambda e: e.scalar_tensor_tensor(out=obf[:, 0:N], in0=fA[:, 0:N], scalar=gcol, in1=fB[:, 0:N],
                                                   op0=ALU.mult, op1=ALU.mult), r=["fA", "fB", "gcol"], w=["obf"])
            P.dma("pool", ag_o_in_a[0:128, col0:col0 + N], obf[:, 0:N], r=["obf"], w=["ag_o_in"])

        def finalize_B(N, col0, bo, bl):
            P.dve(lambda e: e.reciprocal(out=fA[0:64, 0:N], in_=pb[bl][0:64, 0:N]), w=[PB(bl), "fA"])
            P.dve(lambda e: e.tensor_tensor(out=obf[0:64, 0:N], in0=pb[bo][0:64, 0:N], in1=fA[0:64, 0:N],
                                            op=ALU.mult), r=["fA"], w=[PB(bo), "obf"])
            P.dma("pool", ag_o_in_a[128:192, col0:col0 + N], obf[0:64, 0:N], r=["obf"], w=["ag_o_in"])

        for st in range(NST):
            b, i = st // STB, st % STB
            hsl = st % 2
            rk, c0 = st // TPR, (st % TPR) * 512
            P.dma("sp", hTs[hsl][:, :, :], ag_h_out_v[:, rk, :, c0:c0 + 512], r=["ag_h_out"], w=[("hTs", hsl)])
            half = i % 2
            inproj_mm(hsl, 0, 0)
            for s in range(4):
                kt = i * 4 + s
                tok0 = st * 512 + s * 128
                if s + 1 < 4:
                    inproj_mm(hsl, (s + 1) * 128, (s + 1) % 2)
                zs, kn32, kz, kkn = inproj_chain(s % 2)
                P.dma("pool", ak_p[l, tok0:tok0 + 128, :], kn32[:, 128:256], r=[kkn])
                P.dma("pool", av_p[l, tok0:tok0 + 128, :], zs[:, 256:384], r=[kz])
                if i == STB - 1:
                    P.dma("pool", bk_p[l, b, s * 128:(s + 1) * 128, :], kn32[:, 448:512], r=[kkn])
                    P.dma("pool", bv_p[l, b, s * 128:(s + 1) * 128, :], zs[:, 512:576], r=[kz])
                transposes([
                    (QaT[0][0:64, s * 128:(s + 1) * 128], ("QaT", 0)),
                    (QaT[1][0:64, s * 128:(s + 1) * 128], ("QaT", 1)),
                    (KaT[0][0:64, kt * 128:(kt + 1) * 128], ("KaT", 0)),
                    (KaT[1][0:64, kt * 128:(kt + 1) * 128], ("KaT", 1)),
                    (QbT[0:64, s * 128:(s + 1) * 128], "QbT"),
                    (KbT[0:64, (half * 4 + s) * 128:(half * 4 + s + 1) * 128], "KbT"),
                ], kn32, kkn)
                P.pool(lambda e, kt=kt: e.tensor_copy(out=Va[:, kt, :], in_=zs[:, 256:384]), r=[kz], w=["Va"])
                P.pool(lambda e, s=s: e.tensor_copy(out=Vb[:, half * 4 + s, :], in_=zs[:, 512:576]),
                       r=[kz], w=["Vb"])
            tl = list(range(4, 8)) if i == 0 else list(range(8))
            for n_, t in enumerate(tl):
                rt = ((1 - half) * 4 + t) if t < 4 else (half * 4 + t - 4)
                bk = n_ % 2
                P.pe(lambda e, rt=rt, bk=bk: e.matmul(pb[bk][:, :], lhsT=KbT[0:64, rt * 128:(rt + 1) * 128],
                                                     rhs=QbT[0:64, :], start=True, stop=True),
                     r=["KbT", "QbT"], w=[PB(bk)])
                P.dve(lambda e, t=t, bk=bk: e.scalar_tensor_tensor(out=sb32[:, :], in0=pb[bk][:, :], scalar=0.125,
                                                                   in1=bblkt[:, (7 - t) * 128:(7 - t) * 128 + 512],
                                                                   op0=ALU.mult, op1=ALU.add),
                      r=["bblkt"], w=[PB(bk), "sb32"])
                P.act(lambda e, bk=bk: e.activation(out=Pbb[bk][:, :], in_=sb32[:, :], func=AF.Exp),
                      r=["sb32"], w=[("Pb", bk)])
                P.pe(lambda e, rt=rt, bk=bk, n_=n_: e.matmul(pb[2][0:64, :], lhsT=Vb[:, rt, :], rhs=Pbb[bk][:, :],
                                                            start=(n_ == 0), stop=(n_ == len(tl) - 1)),
                     r=["Vb", ("Pb", bk)], w=[PB(2)])
                P.pe(lambda e, bk=bk, n_=n_: e.matmul(pb[3][0:64, :], lhsT=onesb[:, 0:64], rhs=Pbb[bk][:, :],
                                                     start=(n_ == 0), stop=(n_ == len(tl) - 1)),
                     r=[onesb.name, ("Pb", bk)], w=[PB(3)])
            finalize_B(512, st * 512, 2, 3)
            nkt = 4 * (i + 1)

            def qk(kt):
                j = kt - 4 * i
                for m in range(2):
                    P.pe(lambda e, m=m, kt=kt, j=j: e.matmul(pb[m][:, :], lhsT=KaT[m][0:67, kt * 128:(kt + 1) * 128],
                                                            rhs=QaT[m][0:67, :], start=True, stop=(j < 0)),
                         r=[("KaT", m), ("QaT", m)], w=[PB(m)])
                    if j >= 0:
                        P.pe(lambda e, m=m, j=j: e.matmul(pb[m][:, :], lhsT=identb[:, :],
                                                         rhs=cblk[:, (3 - j) * 128:(3 - j) * 128 + 512],
                                                         start=False, stop=True),
                             r=[identb.name, cblk.name], w=[PB(m)])

            qk(0)
            for kt in range(nkt):
                sl = kt % 2
                mm = 4 * i - kt
                for m in range(2):
                    P.act(lambda e, m=m, sl=sl, mm=mm: e.activation(out=Pa[m][sl][:, :], in_=pb[m][:, :], func=AF.Exp,
                                                                   bias=biasA[:, mm + 3:mm + 4], scale=0.125),
                          r=[biasA.name], w=[PB(m), ("Pa", m, sl)])
                if kt + 1 < nkt:
                    qk(kt + 1)
                for m in range(2):
                    P.pe(lambda e, m=m, sl=sl, kt=kt: e.matmul(pb[2 + m][:, :], lhsT=Va[:, kt, :], rhs=Pa[m][sl][:, :],
                                                              start=(kt == 0), stop=(kt == nkt - 1)),
                         r=["Va", ("Pa", m, sl)], w=[PB(2 + m)])
                    P.pe(lambda e, m=m, sl=sl, kt=kt: e.matmul(pb[4 + m][:, :], lhsT=onesb[:, :], rhs=Pa[m][sl][:, :],
                                                              start=(kt == 0), stop=(kt == nkt - 1)),
                         r=[onesb.name, ("Pa", m, sl)], w=[PB(4 + m)])
            finalize_A(512, st * 512)

        KsT, Vs, ck, cv = T1["KsT"], T1["Vs"], T1["ck"], T1["cv"]
        QsT, KnT, QbsT, KbnT, KbsT, Vbs = T1["QsT"], T1["KnT"], T1["QbsT"], T1["KbnT"], T1["KbsT"], T1["Vbs"]
        cbkt, cbvt, vas, vbs, Vsn, Vbsn = T1["cbkt"], T1["cbvt"], T1["vas"], T1["vbs"], T1["Vsn"], T1["Vbsn"]
        for u in range(2):
            hsl = u
            for rr in range(4):
                P.dma("sp", hTs[hsl][:, :, rr * 32:(rr + 1) * 32], ag_h_out_v[:, 4 * u + rr, :, TPC:TPC + 32],
                      r=["ag_h_out"], w=[("hTs", hsl)])
            inproj_mm(hsl, 0, u)
            zs, kn32, kz, kkn = inproj_chain(u)
            ts0 = u * 128
            P.dma("pool", ak_s[l, ts0:ts0 + 128, :], kn32[:, 128:256], r=[kkn])
            P.dma("pool", av_s[l, ts0:ts0 + 128, :], zs[:, 256:384], r=[kz])
            P.dma("pool", bk_s[l, ts0:ts0 + 128, :], kn32[:, 448:512], r=[kkn])
            P.dma("pool", bv_s[l, ts0:ts0 + 128, :], zs[:, 512:576], r=[kz])
            transposes([
                (QsT[0][0:64, ts0:ts0 + 128], ("QsT", 0)), (QsT[1][0:64, ts0:ts0 + 128], ("QsT", 1)),
                (KnT[0][0:64, ts0:ts0 + 128], ("KnT", 0)), (KnT[1][0:64, ts0:ts0 + 128], ("KnT", 1)),
                (QbsT[0:64, ts0:ts0 + 128], "QbsT"), (KbnT[0:64, ts0:ts0 + 128], "KbnT"),
            ], kn32, kkn)
            P.pool(lambda e, u=u: e.tensor_copy(out=vas[u][:, :], in_=zs[:, 256:384]), r=[kz], w=[("vas", u)])
            P.pool(lambda e, u=u: e.tensor_copy(out=vbs[u][:, :], in_=zs[:, 512:576]), r=[kz], w=[("vbs", u)])
        for bs in range(16):
            u, o16 = bs // 8, (bs % 8) * 16
            q0c = bs * 16
            for hf in range(max(1, NCT // 8)):
                nt8 = min(8, NCT)
                P.dma("sp", ck[:, 0:nt8, :], cak[l, bs, hf * 1024:hf * 1024 + nt8 * 128, :].rearrange(
                    "(t p) d -> p t d", p=128), w=["ck"])
                P.dma("sp", cv[:, 0:nt8, :], cav[l, bs, hf * 1024:hf * 1024 + nt8 * 128, :].rearrange(
                    "(t p) d -> p t d", p=128), w=["cv"])
                for t4 in range(nt8 // 4):
                    for m in range(2):
                        for tt in range(4):
                            t = t4 * 4 + tt
                            P.pe(lambda e, m=m, t=t, tt=tt: e.transpose(out=pb[m][0:64, tt * 128:(tt + 1) * 128],
                                                                       in_=ck[:, t, m * 64:(m + 1) * 64],
                                                                       identity=identf[:, :]),
                                 r=["ck", identf.name], w=[PB(m)])
                        kc = (hf * 8 + t4 * 4) * 128
                        if m == 0:
                            P.act(lambda e, kc=kc: e.copy(out=KsT[0][0:64, kc:kc + 512], in_=pb[0][0:64, :]),
                                  w=[PB(0), ("KsT", 0)])
                        else:
                            P.dve(lambda e, kc=kc: e.tensor_copy(out=KsT[1][0:64, kc:kc + 512], in_=pb[1][0:64, :]),
                                  w=[PB(1), ("KsT", 1)])
                P.pool(lambda e, hf=hf, nt8=nt8: e.tensor_copy(out=Vs[:, hf * 8:hf * 8 + nt8, :], in_=cv[:, 0:nt8, :]),
                       r=["cv"], w=["Vs"])
            kn0 = NCT * 128
            P.act(lambda e, q0c=q0c: e.copy(out=KsT[0][0:64, kn0:kn0 + 16], in_=KnT[0][0:64, q0c:q0c + 16]),
                  r=[("KnT", 0)], w=[("KsT", 0)])
            P.dve(lambda e, q0c=q0c: e.tensor_copy(out=KsT[1][0:64, kn0:kn0 + 16], in_=KnT[1][0:64, q0c:q0c + 16]),
                  r=[("KnT", 1)], w=[("KsT", 1)])
            P.dma("sp", Vsn[:, :], vas[u][o16:o16 + 16, :], r=[("vas", u)], w=["Vsn"])
            for kt in range(NCT + 1):
                last = kt == NCT
                nk = 16 if last else 128
                sl = kt % 2
                for m in range(2):
                    P.pe(lambda e, m=m, kt=kt, nk=nk, last=last: e.matmul(
                        pb[m][0:nk, 0:16], lhsT=KsT[m][0:67, kt * 128:kt * 128 + nk], rhs=QsT[m][0:67, q0c:q0c + 16],
                        start=True, stop=not last), r=[("KsT", m), ("QsT", m)], w=[PB(m)])
                    if last:
                        P.pe(lambda e, m=m: e.matmul(pb[m][0:16, 0:16], lhsT=identb[0:16, 0:16],
                                                     rhs=cblk[0:16, 3 * 128:3 * 128 + 16], start=False, stop=True),
                             r=[identb.name, cblk.name], w=[PB(m)])
                mm = 0 if last else NCT - kt
                for m in range(2):
                    P.act(lambda e, m=m, sl=sl, mm=mm, nk=nk: e.activation(
                        out=Pa[m][sl][0:nk, 0:16], in_=pb[m][0:nk, 0:16], func=AF.Exp,
                        bias=biasA[0:nk, mm + 3:mm + 4], scale=0.125), r=[biasA.name], w=[PB(m), ("Pa", m, sl)])
                for m in range(2):
                    vl = Vsn[0:16, :] if last else Vs[:, kt, :]
                    P.pe(lambda e, m=m, sl=sl, kt=kt, nk=nk, vl=vl: e.matmul(
                        pb[2 + m][:, q0c:q0c + 16], lhsT=vl, rhs=Pa[m][sl][0:nk, 0:16],
                        start=(kt == 0), stop=last), r=["Vs", "Vsn", ("Pa", m, sl)], w=[PB(2 + m)])
                    P.pe(lambda e, m=m, sl=sl, kt=kt, nk=nk: e.matmul(
                        pb[4 + m][:, q0c:q0c + 16], lhsT=onesb[0:nk, :], rhs=Pa[m][sl][0:nk, 0:16],
                        start=(kt == 0), stop=last), r=[onesb.name, ("Pa", m, sl)], w=[PB(4 + m)])
        finalize_A(256, NTOK)
        for bs in range(16):
            u, o16 = bs // 8, (bs % 8) * 16
            q0c = bs * 16
            P.dma("sp", cbkt[:, :, :], cbk[l, bs].rearrange("(t p) d -> p t d", p=128), w=["cbkt"])
            P.dma("sp", cbvt[:, :, :], cbv[l, bs].rearrange("(t p) d -> p t d", p=128), w=["cbvt"])
            for tt in range(4):
                P.pe(lambda e, tt=tt: e.transpose(out=pb[0][0:64, tt * 128:(tt + 1) * 128], in_=cbkt[:, tt, :],
                                                  identity=identf[:, :]), r=["cbkt", identf.name], w=[PB(0)])
            P.act(lambda e: e.copy(out=KbsT[0:64, 0:512], in_=pb[0][0:64, :]), w=[PB(0), "KbsT"])
            P.dve(lambda e, q0c=q0c: e.tensor_copy(out=KbsT[0:64, 512:528], in_=KbnT[0:64, q0c:q0c + 16]),
                  r=["KbnT"], w=["KbsT"])
            P.pool(lambda e: e.tensor_copy(out=Vbs[:, 0:4, :], in_=cbvt[:, :, :]), r=["cbvt"], w=["Vbs"])
            P.dma("sp", Vbsn[:, :], vbs[u][o16:o16 + 16, :], r=[("vbs", u)], w=["Vbsn"])
            for t in range(5):
                last = t == 4
                nk = 16 if last else 128
                sl = t % 2
                P.pe(lambda e, t=t, nk=nk: e.matmul(pb[1][0:nk, 0:16], lhsT=KbsT[0:64, t * 128:t * 128 + nk],
                                                   rhs=QbsT[0:64, q0c:q0c + 16], start=True, stop=True),
                     r=["KbsT", "QbsT"], w=[PB(1)])
                P.dve(lambda e, t=t, nk=nk: e.scalar_tensor_tensor(out=sb32[0:nk, 0:16], in0=pb[1][0:nk, 0:16],
                                                                   scalar=0.125, in1=bblkst[0:nk, t * 16:(t + 1) * 16],
                                                                   op0=ALU.mult, op1=ALU.add),
                      r=["bblkst"], w=[PB(1), "sb32"])
                P.act(lambda e, sl=sl, nk=nk: e.activation(out=Pbb[sl][0:nk, 0:16], in_=sb32[0:nk, 0:16], func=AF.Exp),
                      r=["sb32"], w=[("Pb", sl)])
                vl = Vbsn[0:16, :] if last else Vbs[:, t, :]
                P.pe(lambda e, t=t, sl=sl, nk=nk, vl=vl: e.matmul(pb[6][0:64, q0c:q0c + 16], lhsT=vl,
                                                                 rhs=Pbb[sl][0:nk, 0:16], start=(t == 0), stop=last),
                     r=["Vbs", "Vbsn", ("Pb", sl)], w=[PB(6)])
                P.pe(lambda e, t=t, sl=sl, nk=nk: e.matmul(pb[7][0:64, q0c:q0c + 16], lhsT=onesb[0:nk, 0:64],
                                                          rhs=Pbb[sl][0:nk, 0:16], start=(t == 0), stop=last),
                     r=[onesb.name, ("Pb", sl)], w=[PB(7)])
        finalize_B(256, NTOK, 6, 7)

    def p2(l):
        last_layer = l == L - 1
        P.dma("sp", g2t[:], g2b[l], w=[g2t.name])
        if not last_layer:
            P.dma("sp", g1t[:], g1b[l + 1], w=[g1t.name])
        wslot = [0]

        def wload(src_v, np_, shape3):
            i = wslot[0] % 4
            wslot[0] += 1
            a, c = shape3
            assert a * c <= 8192
            v = wring[i][0:np_, 0:a * c].rearrange("p (a c) -> p a c", a=a)
            h = a // 2
            P.dma("sp", v[:, 0:h, :], src_v[:, 0:h, :], w=[("wr", i, 0)])
            P.dma("sp", v[:, h:a, :], src_v[:, h:a, :], w=[("wr", i, 1)])
            return v, ("wr", i)

        def dump():
            if DBG:
                P.dma("sp", dbg_xt, xt[:, :, :].rearrange("p a b -> p (a b)"), r=["xt"])
                P.dma("sp", dbg_mT, mT[:, :, :].rearrange("p a b -> p (a b)"), r=["mT"])
                P.dma("sp", dbg_gT, gT[:, :, :].rearrange("p a b -> p (a b)"), r=["gT"])
                P.dma("sp", dbg_oaT, oaT[:, :, :].rearrange("p a b -> p (a b)"), r=["oaT"])
                P.dma("sp", dbg_obT, obT[:, :, :].rearrange("p a b -> p (a b)"), r=["obT"])

        STILE = cfg.get("STILE", 0)
        for ti, (c0, N) in enumerate(my_tiles()):
            STOP = cfg.get("STOP", "") if ti == STILE else ""
            nsub, n = load_x(l, c0, N)
            P.dma("sp", hT[:, :, 0:N], ag_h_in_v[:, :, c0:c0 + N], r=["ag_h_in"], w=["hT"])
            if c0 < TPC:
                dyn = lambda e, c0=c0, N=N: bass.ds(P.pid * TPC + c0, N)
            else:
                dyn = lambda e, N=N: bass.ds(P.pid * 32 + NTOK, N)
            P.add("sp", lambda e, dyn=dyn, N=N: e.dma_start(out=oaT[:, :, 0:N], in_=ag_o_out_v[0:128, :, dyn(e)]),
                  r=["ag_o_out"], w=["oaT"], kind="d", late=True)
            P.add("sp", lambda e, dyn=dyn, N=N: e.dma_start(out=obT[:, :, 0:N], in_=ag_o_out_v[128:192, :, dyn(e)]),
                  r=["ag_o_out"], w=["obT"], kind="d", late=True)
            for half in range(2):
                wv, wk = wload(wg_s[l][:, :, half * 1024:(half + 1) * 1024], 128, (8, 1024))
                for gc in range(8):
                    bk = nbank(0, 4)
                    for j in range(8):
                        P.pe(lambda e, wv=wv, gc=gc, j=j, bk=bk: e.matmul(
                            pb[bk][:, 0:N], lhsT=wv[:, j, gc * 128:(gc + 1) * 128], rhs=hT[:, j, 0:N],
                            start=(j == 0), stop=(j == 7)), r=[wk + (0,), wk + (1,), "hT"], w=[PB(bk)])
                    P.act(lambda e, bk=bk, half=half, gc=gc: e.activation(out=gT[:, half * 8 + gc, 0:N],
                                                                        in_=pb[bk][:, 0:N], func=AF.Sigmoid),
                          w=[PB(bk), "gT"])
            if STOP == "p2b":
                return dump()
            wav, wak = wload(wa_s[l], 128, (8, 1024))
            wbv, wbk = wload(wb_s[l], 64, (8, 1024))
            for oc in range(8):
                ba, bb = nbank(4, 8), nbank(4, 8)
                for h in range(8):
                    P.pe(lambda e, h=h, oc=oc, ba=ba: e.matmul(pb[ba][:, 0:N], lhsT=wav[:, h, oc * 128:(oc + 1) * 128],
                                                              rhs=oaT[:, h, 0:N], start=(h == 0), stop=(h == 7)),
                         r=[wak + (0,), wak + (1,), "oaT"], w=[PB(ba)])
                for h in range(8):
                    P.pe(lambda e, h=h, oc=oc, bb=bb: e.matmul(pb[bb][:, 0:N], lhsT=wbv[0:64, h, oc * 128:(oc + 1) * 128],
                                                              rhs=obT[0:64, h, 0:N], start=(h == 0), stop=(h == 7)),
                         r=[wbk + (0,), wbk + (1,), "obT"], w=[PB(bb)])
                P.dve(lambda e, oc=oc, ba=ba: e.tensor_tensor(out=tA[:, 0:N], in0=pb[ba][:, 0:N], in1=gT[:, oc, 0:N],
                                                             op=ALU.mult), r=["gT"], w=[PB(ba), "tA"])
                P.dve(lambda e, oc=oc, bb=bb: e.tensor_tensor(out=tB[:, 0:N], in0=pb[bb][:, 0:N],
                                                             in1=gT[:, 8 + oc, 0:N], op=ALU.mult),
                      r=["gT"], w=[PB(bb), "tB"])
                P.pool(lambda e, oc=oc: e.tensor_tensor(out=mT[:, oc, 0:N], in0=tA[:, 0:N], in1=tB[:, 0:N], op=ALU.add),
                       r=["tA", "tB"], w=["mT"])
            if STOP == "p2c":
                return dump()
            wov, wok = wload(wo_s[l], 128, (8, 1024))
            for s in range(nsub):
                for hc in range(2):
                    bk = nbank(0, 4)
                    for oc in range(8):
                        P.pe(lambda e, s=s, hc=hc, oc=oc, bk=bk: e.matmul(
                            pb[bk][0:n, :], lhsT=mT[:, oc, s * 128:s * 128 + n], rhs=wov[:, oc, hc * 512:(hc + 1) * 512],
                            start=(oc == 0), stop=(oc == 7)), r=[wok + (0,), wok + (1,), "mT"], w=[PB(bk)])
                    P.dve(lambda e, s=s, hc=hc, bk=bk: e.tensor_tensor(
                        out=xt[0:n, s, hc * 512:(hc + 1) * 512], in0=xt[0:n, s, hc * 512:(hc + 1) * 512],
                        in1=pb[bk][0:n, :], op=ALU.add), w=[PB(bk), "xt"])
            if STOP == "p2d":
                return dump()
            for s in range(nsub):
                norm_transpose(xt[0:n, s, :], n, g2t, h2T[:, :, s * 128:s * 128 + n], "h2T", "xt")
            if STOP == "p2e":
                return dump()
            for q in range(4):
                wv, wk = wload(w1_s[l][:, :, q * 1024:(q + 1) * 1024], 128, (8, 1024))
                for fcl in range(8):
                    fc = q * 8 + fcl
                    bk = nbank(0, 4)
                    for j in range(8):
                        P.pe(lambda e, wv=wv, fcl=fcl, j=j, bk=bk: e.matmul(
                            pb[bk][:, 0:N], lhsT=wv[:, j, fcl * 128:(fcl + 1) * 128], rhs=h2T[:, j, 0:N],
                            start=(j == 0), stop=(j == 7)), r=[wk + (0,), wk + (1,), "h2T"], w=[PB(bk)])
                    ri = fc % 2
                    P.act(lambda e, bk=bk, ri=ri: e.activation(out=rbuf[ri][:, 0:N], in_=pb[bk][:, 0:N], func=AF.Relu),
                          w=[PB(bk), ("rbuf", ri)])
                    P.pool(lambda e, fc=fc, ri=ri: e.tensor_tensor(out=u2T[:, fc, 0:N], in0=rbuf[ri][:, 0:N],
                                                                  in1=rbuf[ri][:, 0:N], op=ALU.mult),
                           r=[("rbuf", ri)], w=["u2T"])
            if STOP == "p2f":
                return dump()
            for hc in range(2):
                wvs = [wload(w2_s[l][:, q * 16:(q + 1) * 16, hc * 512:(hc + 1) * 512], 128, (16, 512)) for q in range(2)]
                for s in range(nsub):
                    bk = nbank(4, 8)
                    for fc in range(32):
                        wv, wk = wvs[fc // 16]
                        P.pe(lambda e, wv=wv, s=s, fc=fc, bk=bk: e.matmul(
                            pb[bk][0:n, :], lhsT=u2T[:, fc, s * 128:s * 128 + n], rhs=wv[:, fc % 16, :],
                            start=(fc == 0), stop=(fc == 31)), r=[wk + (0,), wk + (1,), "u2T"], w=[PB(bk)])
                    P.dve(lambda e, s=s, hc=hc, bk=bk: e.tensor_tensor(
                        out=xt[0:n, s, hc * 512:(hc + 1) * 512], in0=xt[0:n, s, hc * 512:(hc + 1) * 512],
                        in1=pb[bk][0:n, :], op=ALU.add), w=[PB(bk), "xt"])
            if STOP == "p2g":
                return dump()
            if last_layer:
                if c0 < TPC:
                    P.dma("pool", y_p[c0:c0 + N, :].rearrange("(s p) d -> p s d", p=128), xt[:, 0:nsub, :], r=["xt"])
                else:
                    P.dma("pool", y_s[:, :], xt[0:n, 0, :], r=["xt"])
            else:
                if N >= 128:
                    P.dma("pool", x1[c0:c0 + N, :].rearrange("(s p) d -> p s d", p=128), xt[:, 0:nsub, :],
                          r=["xt"], w=["x1"])
                else:
                    P.dma("pool", x1[c0:c0 + N, :], xt[0:n, 0, :], r=["xt"], w=["x1"])
                emit_hT(l + 1, c0, N, nsub, n)
            if STOP == "p2h":
                return dump()

    for l in range(L):
        P.barrier(bscr[:])
        p1(l)
        if DBG and l == 0:
            P.dma("sp", dbg_o, ag_o_in_a, r=["ag_o_in"])
        if STOP == "p1":
            break
        allgather(ag_o_in, ag_o_out, "ag_o_in", "ag_o_out")
        if STOP == "ag2":
            break
        P.barrier(bscr[:])
        p2(l)
        if l + 1 < L:
            allgather(ag_h_in, ag_h_out, "ag_h_in", "ag_h_out")
    P.emit()
    return nc


def _consts(cfg, c):
    S, PAST = cfg["SEQ"], cfg["PAST"]
    NKT = S // 128
    bf = ml_dtypes.bfloat16
    slope = 2.0 ** (-(c + 1))
    kk = np.arange(S) % 128
    kaugS = np.stack([8.0 * slope * kk, np.ones(S), np.ones(S)]).astype(np.float32).astype(bf)
    qr = np.arange(512)
    qaug = np.stack([np.ones(512), -8.0 * slope * (qr & 255), -8.0 * slope * (qr & 256)]).astype(np.float32).astype(bf)
    qs = np.arange(256) % 16
    qaugs = np.stack([np.ones(256), -8.0 * slope * qs, np.zeros(256)]).astype(np.float32).astype(bf)
    m = np.arange(NKT + 3) - 3
    biasA = np.broadcast_to((-slope * 128.0 * m)[None, :], (128, NKT + 3)).astype(np.float32).copy()
    kr = np.arange(128)[:, None]
    qq = np.arange(128)[None, :]
    cb = np.zeros((128, 7, 128), np.float32)
    for d in range(-3, 4):
        if d < 0:
            cb[:, d + 3, :] = -240000.0
        elif d == 0:
            same = (kr // 64) == (qq // 64)
            later = (kr // 64) > (qq // 64)
            blk = np.where(later, -240000.0, np.where(same & (kr > qq), -16.0 * slope * (kr - qq), 0.0))
            cb[:, 3, :] = blk
    cblk = cb.reshape(128, 7 * 128).astype(bf)
    eye = np.eye(128, dtype=np.float32)
    one = np.ones((128, 128), np.float32)
    return dict(kaugS=kaugS, qaug=qaug, qaugs=qaugs, biasA=biasA, cblk=cblk, identb=eye.astype(bf), identf=eye,
                onesb=one.astype(bf), onesf=one)


def _bblocks(rel, PAST):
    NEG = np.float32(-30000.0)
    kr = np.arange(128)[:, None]
    qq = np.arange(128)[None, :]
    out = np.empty((128, 11, 128), np.float32)
    for d in range(-7, 4):
        delta = 128 * d + 512 + qq - kr
        dch = (128 * d + 512 + qq) // 64 - kr // 64
        vis = (dch >= 0) & (dch <= 8)
        idx = np.clip(delta, -128, 128) + 128
        out[:, d + 7, :] = np.where(vis, rel[idx], NEG)
    sm = np.full((128, 5, 16), NEG, np.float32)
    qi = np.arange(16)[None, :]
    for t in range(5):
        nk = 128 if t < 4 else 16
        kpos = (PAST - 512 + 128 * t + np.arange(nk))[:, None] if t < 4 else (PAST + np.arange(16))[:, None]
        delta = (PAST + qi) - kpos
        dch = (PAST + qi) // 64 - kpos // 64
        vis = (dch >= 0) & (dch <= 8)
        idx = np.clip(delta, -128, 128) + 128
        sm[0:nk, t, :] = np.where(vis, rel[idx], NEG)
    return out.reshape(128, 11 * 128), sm.reshape(128, 80)


def _run(cfg, inp):
    S, PAST, L = cfg["SEQ"], cfg["PAST"], cfg["L"]
    NTOK = 2 * S
    TPC = NTOK // NC
    f = lambda k: np.asarray(inp[k], dtype=np.float32)
    x_prompt = f("x_prompt").reshape(NTOK, D)
    x_sample = f("x_sample").reshape(256, D)
    w_in = f("w_in")
    rep = lambda v: np.ascontiguousarray(np.broadcast_to(v[:, None, :], (v.shape[0], 128, v.shape[1])))
    ones64 = np.ones((L, 64), np.float32)
    in_maps = []
    for c in range(NC):
        cols = np.concatenate([np.arange(128 * c, 128 * c + 128), 1024 + np.arange(128 * c, 128 * c + 128),
                               2048 + np.arange(128 * c, 128 * c + 128), 3072 + np.arange(64 * c, 64 * c + 64),
                               3584 + np.arange(64 * c, 64 * c + 64), 4096 + np.arange(64 * c, 64 * c + 64)])
        qkg = np.concatenate([f("qn_a_g"), f("qn_a_g"), f("kn_a_g"), f("kn_a_g"), ones64, ones64, f("qn_b_g"),
                              f("kn_b_g")], axis=1)
        lamv = np.concatenate([f("lam_q1"), f("lam_k1"), f("lam_q2"), f("lam_k2")], axis=1)
        bb = [_bblocks(f("rel_bias_b")[l, c], PAST) for l in range(L)]
        d = dict(
            xp=np.ascontiguousarray(x_prompt[c * TPC:(c + 1) * TPC]),
            xs=np.ascontiguousarray(x_sample[c * 32:(c + 1) * 32]),
            win=np.ascontiguousarray(w_in[:, :, cols]),
            g1b=rep(f("norm1_g")), g2b=rep(f("norm2_g")), qkg=rep(qkg), lamv=rep(lamv),
            sublng=np.ascontiguousarray(f("subln_a_g")[:, :, None]),
            bblk=np.stack([b[0] for b in bb]), bblks=np.stack([b[1] for b in bb]),
            cak=np.ascontiguousarray(f("cache_a_k")[:, :, :, c, :]), cav=np.ascontiguousarray(f("cache_a_v")[:, :, :, c, :]),
            cbk=np.ascontiguousarray(f("cache_b_k")[:, :, :, c, :]), cbv=np.ascontiguousarray(f("cache_b_v")[:, :, :, c, :]),
            wbra=f("w_br_a"), wbrb=f("w_br_b"), wgate=f("w_gate"), wout=f("w_out"), wff1=f("w_ff1"), wff2=f("w_ff2"),
        )
        d.update(_consts(cfg, c))
        in_maps.append(d)
    nc = build(cfg)
    res = run_bass_kernel_spmd(nc, in_maps, core_ids=list(range(NC)))
    R = res.results
    if cfg.get("DBG"):
        global DBG_OUT
        DBG_OUT = [{k: np.asarray(v) for k, v in R[c].items() if k.startswith("dbg_")} for c in range(NC)]
    g = lambda k: [np.asarray(R[c][k], dtype=np.float32) for c in range(NC)]
    y_p = np.concatenate(g("y_p"), 0).reshape(2, S, D)
    y_s = np.concatenate(g("y_s"), 0).reshape(16, 16, D)
    hs = lambda k, shp: np.stack(g(k), axis=-2).reshape(shp)
    ak_p = hs("ak_p", (L, 2, S, 8, 128)); av_p = hs("av_p", (L, 2, S, 8, 128))
    bk_p = hs("bk_p", (L, 2, 512, 8, 64)); bv_p = hs("bv_p", (L, 2, 512, 8, 64))
    ak_s = hs("ak_s", (L, 16, 16, 8, 128)); av_s = hs("av_s", (L, 16, 16, 8, 128))
    bk_s = hs("bk_s", (L, 16, 16, 8, 64)); bv_s = hs("bv_s", (L, 16, 16, 8, 64))
    return (y_p, y_s, ak_p, av_p, bk_p, bv_p, ak_s, av_s, bk_s, bv_s)


def kernel(**inputs):
    cfg = dict(SEQ=16384, PAST=2048, L=2)
    return _run(cfg, inputs)
```
